# Optimizing a Trainium2 kernel written in Bass

```python
import jax, jax.numpy as jnp
from jax import lax
import numpy as np

D_MODEL = 2048
BATCH = 4
SEQ = 4096
DEPTH = 4

MIX_WIDTH = D_MODEL
POOL_WIDTH = MIX_WIDTH // 2
RWKV_WIDTH = MIX_WIDTH - POOL_WIDTH
POOL_WINDOWS = (2, 4, 8, 16)
N_POOL_GROUPS = len(POOL_WINDOWS)
POOL_GROUP = POOL_WIDTH // N_POOL_GROUPS
HEAD_SIZE = 64
N_RWKV_HEADS = RWKV_WIDTH // HEAD_SIZE
D_DECAY_LORA = 64
D_AAA_LORA = 64
D_MV_LORA = 32
D_GATE_LORA = 160
D_FF = 4 * D_MODEL
D_PLE = 256
NORM_EPS = 1e-6
GN_EPS = 64e-5
SHIFT_WIDTH = 3 * RWKV_WIDTH + D_DECAY_LORA + D_AAA_LORA + D_GATE_LORA
IN_WIDTH = POOL_WIDTH + SHIFT_WIDTH

kernel_name = "hymba_pool_rwkv7_hybrid"


def rms_norm(x, g):
    xf = x.astype(jnp.float32)
    y = xf * lax.rsqrt(jnp.mean(xf * xf, axis=-1, keepdims=True) + NORM_EPS)
    return (y * g.astype(jnp.float32)).astype(x.dtype)


def token_shift(z, mu):
    zf = z.astype(jnp.float32)
    z_prev = jnp.pad(zf, ((0, 0), (1, 0), (0, 0)))[:, :-1]
    return zf + (z_prev - zf) * mu.astype(jnp.float32)


def causal_multiscale_pool(u):
    B, T, _ = u.shape
    uf = u.astype(jnp.float32).reshape(B, T, N_POOL_GROUPS, POOL_GROUP)
    cs = jnp.cumsum(uf, axis=1)
    t = jnp.arange(T)
    outs = []
    for gi, win in enumerate(POOL_WINDOWS):
        c = cs[:, :, gi]
        lagged = jnp.pad(c, ((0, 0), (win, 0), (0, 0)))[:, :T]
        cnt = jnp.minimum(t + 1, win).astype(jnp.float32)[None, :, None]
        outs.append((c - lagged) / cnt - uf[:, :, gi])
    return jnp.stack(outs, axis=2)


def wkv7_scan(r, decay, k, v, a_vec, b_vec):
    B, T, H, N = r.shape
    xs = tuple(jnp.swapaxes(z, 0, 1) for z in (r, decay, k, v, a_vec, b_vec))

    def step(S, inp):
        r_t, w_t, k_t, v_t, a_t, b_t = inp
        sa = jnp.einsum('bhij,bhj->bhi', S, a_t)
        S = S * w_t[:, :, None, :] + sa[..., None] * b_t[:, :, None, :] + v_t[..., None] * k_t[:, :, None, :]
        y = jnp.einsum('bhij,bhj->bhi', S, r_t)
        return S, y

    S0 = jnp.zeros((B, H, N, N), jnp.float32)
    _, y = lax.scan(step, S0, xs)
    return jnp.swapaxes(y, 0, 1)


def rwkv7_time_mix(zr, zk, zv, zw, za, zg, zvr, v_first, w0, w_up, a0, a_up, g_up,
                   v0, v_up, k_k, k_a, r_k, gn_g, gn_b):
    f32 = jnp.float32
    B, T, _ = zr.shape
    H, N = N_RWKV_HEADS, HEAD_SIZE
    r = zr
    k = zk
    v = zv
    w = -jax.nn.softplus(-(w0.astype(f32) + jnp.tanh(zw) @ w_up.astype(f32))) - 0.5
    decay = jnp.exp(-jnp.exp(w))
    a = jax.nn.sigmoid(a0.astype(f32) + za @ a_up.astype(f32))
    g = jax.nn.sigmoid(zg) @ g_up.astype(f32)
    if zvr is None:
        v_first = v
    else:
        v = v + (v_first - v) * jax.nn.sigmoid(v0.astype(f32) + zvr @ v_up.astype(f32))
    hs = lambda z: z.reshape(B, T, H, N)
    kk = hs(k * k_k.astype(f32))
    kk = kk / jnp.maximum(jnp.linalg.norm(kk, axis=-1, keepdims=True), 1e-12)
    k = k * (1.0 + (a - 1.0) * k_a.astype(f32))
    r_h, k_h, v_h, a_h = hs(r), hs(k), hs(v), hs(a)
    y = wkv7_scan(r_h, hs(decay), k_h, v_h, -kk, kk * a_h)
    mu = jnp.mean(y, axis=-1, keepdims=True)
    var = jnp.mean(jnp.square(y - mu), axis=-1, keepdims=True)
    y = (y - mu) * lax.rsqrt(var + GN_EPS) * gn_g.astype(f32).reshape(H, N) + gn_b.astype(f32).reshape(H, N)
    y = y + jnp.sum(r_h * k_h * r_k.astype(f32), axis=-1, keepdims=True) * v_h
    return y.reshape(B, T, RWKV_WIDTH) * g, v_first


def setup_inputs(seed: int = 0) -> dict:
    key = jax.random.key(seed)
    ks = iter(jax.random.split(key, 40))
    f32 = jnp.float32

    def nrm(shape, scale):
        return jax.random.normal(next(ks), shape, f32) * scale

    def unif(shape, lo, hi):
        return jax.random.uniform(next(ks), shape, f32, minval=lo, maxval=hi)

    L, Lm1, D, R = DEPTH, DEPTH - 1, D_MODEL, RWKV_WIDTH
    return {
        "x": nrm((BATCH, SEQ, D), 1.0),
        "p": nrm((DEPTH, BATCH, SEQ, D_PLE), 1.0),
        "attn_norm": 1.0 + nrm((L, D), 0.02),
        "w_in": nrm((L, D, IN_WIDTH), D ** -0.5),
        "mu_shift": unif((L, SHIFT_WIDTH), 0.0, 1.0),
        "w_vres_dn": nrm((Lm1, D, D_MV_LORA), D ** -0.5),
        "mu_vres": unif((Lm1, D_MV_LORA), 0.0, 1.0),
        "v0": nrm((Lm1, R), 0.5),
        "v_up": nrm((Lm1, D_MV_LORA, R), D_MV_LORA ** -0.5),
        "pool_w": nrm((L, N_POOL_GROUPS, POOL_GROUP, POOL_GROUP), POOL_GROUP ** -0.5),
        "pool_scale": 0.5 + nrm((L, POOL_WIDTH), 0.1),
        "w0": unif((L, R), -4.0, 1.0),
        "w_up": nrm((L, D_DECAY_LORA, R), D_DECAY_LORA ** -0.5),
        "a0": nrm((L, R), 0.3),
        "a_up": nrm((L, D_AAA_LORA, R), D_AAA_LORA ** -0.5),
        "g_up": nrm((L, D_GATE_LORA, R), D_GATE_LORA ** -0.5),
        "k_k": 0.85 + nrm((L, R), 0.05),
        "k_a": 1.0 + nrm((L, R), 0.05),
        "r_k": nrm((L, N_RWKV_HEADS, HEAD_SIZE), 0.1),
        "gn_g": 1.0 + nrm((L, R), 0.02),
        "gn_b": nrm((L, R), 0.02),
        "w_out": nrm((L, MIX_WIDTH, D), MIX_WIDTH ** -0.5),
        "mlp_norm": 1.0 + nrm((L, D), 0.02),
        "w_ffn_up": nrm((L, D, D_FF), D ** -0.5),
        "w_ffn_down": nrm((L, D_FF, D), D_FF ** -0.5),
        "ple_norm": 1.0 + nrm((L, D), 0.02),
        "w_ple_gate": nrm((L, D, D), D ** -0.5),
        "w_ple_proj": nrm((L, D_PLE, D), D_PLE ** -0.5),
        "final_norm": 1.0 + nrm((D,), 0.02),
    }


def reference(x, p, attn_norm, w_in, mu_shift, w_vres_dn, mu_vres, v0, v_up, pool_w, pool_scale,
              w0, w_up, a0, a_up, g_up, k_k, k_a, r_k, gn_g, gn_b, w_out, mlp_norm, w_ffn_up,
              w_ffn_down, ple_norm, w_ple_gate, w_ple_proj, final_norm):
    B, T, _ = x.shape
    R = RWKV_WIDTH
    o_w = 3 * R
    o_a = o_w + D_DECAY_LORA
    o_g = o_a + D_AAA_LORA
    v_first = None
    for i in range(DEPTH):
        h = rms_norm(x, attn_norm[i])
        if i == 0:
            z = h @ w_in[0]
        else:
            z = h @ jnp.concatenate([w_in[i], w_vres_dn[i - 1]], axis=1)
        z_pool = z[..., :POOL_WIDTH]
        zs = token_shift(z[..., POOL_WIDTH:IN_WIDTH], mu_shift[i])
        zvr = None if i == 0 else token_shift(z[..., IN_WIDTH:], mu_vres[i - 1])

        d = causal_multiscale_pool(z_pool)
        pool_out = jnp.einsum('btgc,gcd->btgd', d, pool_w[i].astype(jnp.float32)).reshape(B, T, POOL_WIDTH)
        pool_out = pool_out * pool_scale[i].astype(jnp.float32)

        rwkv_out, v_first = rwkv7_time_mix(
            zs[..., :R], zs[..., R:2 * R], zs[..., 2 * R:o_w], zs[..., o_w:o_a], zs[..., o_a:o_g], zs[..., o_g:],
            zvr, v_first, w0[i], w_up[i], a0[i], a_up[i], g_up[i],
            None if i == 0 else v0[i - 1], None if i == 0 else v_up[i - 1],
            k_k[i], k_a[i], r_k[i], gn_g[i], gn_b[i])

        mix = jnp.concatenate([pool_out, rwkv_out], axis=-1).astype(x.dtype)
        x = x + mix @ w_out[i]

        h2 = rms_norm(x, mlp_norm[i])
        x = x + jnp.square(jax.nn.relu(h2 @ w_ffn_up[i])) @ w_ffn_down[i]

        gate = jax.nn.sigmoid(rms_norm(x, ple_norm[i]) @ w_ple_gate[i])
        x = x + gate * (p[i] @ w_ple_proj[i])
    return rms_norm(x, final_norm)
```

```python
import numpy as np
import concourse.bass as bass
import concourse.mybir as mybir
from concourse.bass_utils import run_bass_kernel_spmd

F32 = mybir.dt.float32
BF16 = mybir.dt.bfloat16
AF = mybir.ActivationFunctionType
ALU = mybir.AluOpType

D = 2048
SEQ = 4096
BATCH = 4
DEPTH = 4
TT = 256
NSUB = TT // 64
NTB = TT // 128
NCH = 16
C0 = float(np.exp(-0.5))
NORM_EPS = 1e-6
GN_EPS = 64e-5
WSLOT = 8192
NWS = 2

V_ATTN, V_MLP, V_PLE, V_FIN = 0, 16, 32, 48
V_MU = 64
V_OMM = 92
V_PSC = 120
V_W0, V_A0, V_V0, V_KK, V_KA, V_OKA, V_RK = 128, 136, 144, 152, 160, 168, 176
V_GNG, V_GNB = 184, 200
NV = 216

CB_ID = 0
CB_ONESM = 128
CB_BONES = 256
CB_IDB = 384
CB_PMC = 448
CB_PMH = 960
CB_PMF = 1472
CB_SEL = 1984
NCB = 2112
CF_MU_S = 0
CF_MU_I = 64
CF_ML_S = 128
CF_ID64 = 192
CF_RESET = 256
CF_ID2 = CF_RESET + TT
CF_ONES64 = CF_ID2 + 64
CF_EPSN = CF_ONES64 + 64
CF_EPSG = CF_EPSN + 1
NCF = CF_EPSG + 1


class Buf:
    __slots__ = ("w", "r", "name")

    def __init__(self, name=""):
        self.w = None
        self.r = {}
        self.name = name


class Prog:
    ENG = ["pe", "act", "dve", "pool", "sp"]

    def __init__(self, nc):
        self.nc = nc
        self.q = {e: [] for e in self.ENG}
        self.cnt = {e: 0 for e in self.ENG}
        self.sem = {e: nc.alloc_semaphore("s_" + e) for e in self.ENG}
        self.waited = {e: {} for e in self.ENG}
        self.dsem = {}
        self.dcnt = {}

    def _semobj(self, key):
        return self.sem[key] if key in self.sem else self.dsem[key]

    def _waits(self, eng, reads, writes):
        deps = {}
        for b in reads:
            if b.w is not None:
                k, v = b.w
                if deps.get(k, 0) < v:
                    deps[k] = v
        for b in writes:
            if b.w is not None:
                k, v = b.w
                if deps.get(k, 0) < v:
                    deps[k] = v
            for k, v in b.r.items():
                if deps.get(k, 0) < v:
                    deps[k] = v
        waits = []
        wd = self.waited[eng]
        for k, v in deps.items():
            if k == eng and eng == "pe":
                continue
            if wd.get(k, 0) < v:
                waits.append((k, v))
                wd[k] = v
        return waits

    def op(self, eng, fn, reads=(), writes=()):
        waits = self._waits(eng, reads, writes)
        self.cnt[eng] += 1
        c = self.cnt[eng]
        self.q[eng].append((waits, fn, None))
        for b in reads:
            if b.r.get(eng, 0) < c:
                b.r[eng] = c
        for b in writes:
            b.w = (eng, c)
            b.r = {}

    def dma(self, q, key, fns, reads=(), writes=()):
        if key not in self.dsem:
            self.dsem[key] = self.nc.alloc_semaphore("d_" + key)
            self.dcnt[key] = 0
        waits = self._waits(q, reads, writes)
        self.dcnt[key] += 16 * len(fns)
        v = self.dcnt[key]
        for i, fn in enumerate(fns):
            self.q[q].append((waits if i == 0 else [], fn, key))
        for b in reads:
            if b.r.get(key, 0) < v:
                b.r[key] = v
        for b in writes:
            b.w = (key, v)
            b.r = {}

    def barrier(self, engs=("pe", "act", "dve")):
        snap = {e: self.cnt[e] for e in engs}
        for e in engs:
            waits = []
            for e2 in engs:
                if e2 == e or snap[e2] == 0:
                    continue
                if self.waited[e].get(e2, 0) < snap[e2]:
                    waits.append((e2, snap[e2]))
                    self.waited[e][e2] = snap[e2]
            if waits:
                self.q[e].append((waits, None, None))

    def wait_all(self, eng, bufs):
        waits = self._waits(eng, bufs, bufs)
        self.q[eng].append((waits, None, None))

    def emit(self):
        nc = self.nc
        P = self

        def replay(name, e):
            for waits, fn, dkey in P.q[name]:
                for k, v in waits:
                    e.wait_ge(P._semobj(k), v)
                if fn is None:
                    continue
                ins = fn(e)
                if dkey is not None:
                    ins.then_inc(P.dsem[dkey], 16)
                else:
                    ins.then_inc(P.sem[name], 1)

        with nc.Block() as block:

            @block.tensor
            def _(e):
                replay("pe", e)

            @block.scalar
            def _(e):
                replay("act", e)

            @block.vector
            def _(e):
                replay("dve", e)

            @block.gpsimd
            def _(e):
                replay("pool", e)

            @block.sync
            def _(e):
                replay("sp", e)


def build_program(NT, NL, T_in, stop_after=99):
    nc = bass.Bass("TRN2", target_bir_lowering=False)
    dt = nc.dram_tensor
    xT = dt("xT", [NCH, 128, T_in], F32, kind="ExternalInput").ap()
    pT = dt("pT", [DEPTH, 2, 128, T_in], F32, kind="ExternalInput").ap()
    oT = dt("oT", [NCH, 128, T_in], F32, kind="ExternalOutput").ap()
    vecd = dt("vec", [128, DEPTH, NV], F32, kind="ExternalInput").ap()
    cbd = dt("cbf", [128, NCB], F32, kind="ExternalInput").ap()
    cfd = dt("cf32", [128, NCF], F32, kind="ExternalInput").ap()
    wLB = dt("wLB", [NL, 128, 16, 320], F32, kind="ExternalInput").ap()
    wPB = dt("wPB", [NL, 2, 128, 16, 512], F32, kind="ExternalInput").ap()
    wRKV = dt("wRKV", [NL, 8, 128, 16, 384], F32, kind="ExternalInput").ap()
    wSM = dt("wSM", [NL, 128, 6144], F32, kind="ExternalInput").ap()
    wWO = dt("wWO", [NL, 8, 128, 24, 256], F32, kind="ExternalInput").ap()
    wUP = dt("wUP", [NL, 16, 128, 16, 512], F32, kind="ExternalInput").ap()
    wDN = dt("wDN", [NL, 4, 4, 128, 16, 512], F32, kind="ExternalInput").ap()
    wGT = dt("wGT", [NL, 4, 128, 16, 512], F32, kind="ExternalInput").ap()
    wPJ = dt("wPJ", [NL, 128, 2, 2048], F32, kind="ExternalInput").ap()

    P = Prog(nc)
    from contextlib import ExitStack

    with ExitStack() as es:
        def sb(name, shape, dtype):
            return es.enter_context(nc.sbuf_tensor(name, shape, dtype))

        X = sb("X", [128, NCH, TT], F32)
        H = sb("H", [128, NCH, TT], BF16)
        WS = [sb(f"WS{i}", [128, WSLOT], BF16) for i in range(NWS)]
        SMW = sb("SMW", [128, 6144], BF16)
        MIX = sb("MIX", [128, 24, TT], BF16)
        VF = sb("VF", [128, 8, TT], F32)
        SCR = sb("SCR", [128, 64 * TT], BF16)
        VEC = sb("VEC", [128, DEPTH, NV], F32)
        CB = sb("CB", [128, NCB], BF16)
        CF = sb("CF", [128, NCF], F32)
        SQ = sb("SQ", [128, 2, TT], BF16)
        RSTD = sb("RSTD", [128, TT], F32)
        CARRY = sb("CARRY", [128, DEPTH, 28], F32)
        CARRYP = sb("CARRYP", [128, DEPTH, 8, 16], BF16)
        ST = sb("ST", [64, DEPTH, 16, 64], BF16)
        LWA = sb("LWA", [128, TT], BF16)
        LG0 = sb("LG0", [128, TT], BF16)
        LG1 = sb("LG1", [32, TT], BF16)
        LVR = sb("LVR", [32, TT], BF16)
        ZP = sb("ZP", [128, 8, 16 + TT], BF16)
        QC = sb("QC", [128, 1, 512], BF16)
        QH = sb("QH", [16, 1, 512], BF16)
        TMALL = sb("TMALL", [64, 4, NSUB, 4, 128], BF16)
        DGE = sb("DGE", [128, 4, NSUB, 64], BF16)
        RKW = sb("RKW", [128, 8, 64], BF16)
        CH = sb("CH", [64, 14, 8, 64], BF16)
        Y32 = sb("Y32", [64, 8, TT], F32)
        OT = sb("OT", [64, 6, TT], F32)
        PTB = sb("PTB", [128, 2, TT], BF16)
        PJW = sb("PJW", [128, 2, 2048], BF16)
        TBUF = sb("TBUF", [128, 2, TT + 1], F32)
        PS = [es.enter_context(nc.psum_tensor(f"PS{i}", [128, 512], F32)) for i in range(8)]

        bPS = [Buf(f"ps{i}") for i in range(8)]
        bX = [Buf(f"X{c}") for c in range(NCH)]
        bH = [Buf(f"H{c}") for c in range(NCH)]
        bWS = [Buf(f"ws{i}") for i in range(NWS)]
        bSMW = Buf("smw")
        bMIX = [Buf(f"mix{c}") for c in range(24)]
        bVF = [Buf(f"vf{c}") for c in range(8)]
        bVEC, bCB, bCF = Buf("vec"), Buf("cb"), Buf("cf")
        bSQ = [Buf("sq0"), Buf("sq1")]
        bRSTD = Buf("rstd")
        bCARRY = [[Buf(f"cy{l}_{i}") for i in range(28)] for l in range(DEPTH)]
        bCARRYP = [Buf(f"cyp{l}") for l in range(DEPTH)]
        bST = [[Buf(f"st{l}_{h}") for h in range(16)] for l in range(DEPTH)]
        bLWA, bLG0, bLG1, bLVR = Buf("lwa"), Buf("lg0"), Buf("lg1"), Buf("lvr")
        bZP = [Buf(f"zp{c}") for c in range(8)]
        bQC = [Buf("qc0"), Buf("qc1")]
        bQH = [Buf("qh0"), Buf("qh1")]
        bTM = [[Buf(f"tm{m}_{s}") for s in range(NSUB)] for m in range(4)]
        bDGE = [Buf(f"dge{m}") for m in range(4)]
        bRKW = Buf("rkw")
        bCH = [Buf(f"ch{i}") for i in range(14)]
        bY32 = [Buf(f"y{h}") for h in range(8)]
        bOT = [Buf(f"ot{i}") for i in range(8)]
        bPTB = Buf("ptb")
        bPJW = Buf("pjw")
        bOST = [Buf("ost0"), Buf("ost1")]
        bTB = [Buf("tb0"), Buf("tb1")]
        bPHASE = Buf("phase")

        HID = SCR[:, :].rearrange("p (c t) -> p c t", t=TT)
        bHID = [Buf(f"hid{c}") for c in range(64)]
        _scr_off = [0]

        def scr_bf(n):
            o = _scr_off[0]
            _scr_off[0] += n
            return SCR[:, o:o + n]

        def scr_f32(n):
            return scr_bf(2 * n).bitcast(F32)

        NTMP = 18
        TMP = [scr_f32(TT) for _ in range(NTMP)]
        bTMP = [Buf(f"tmp{i}") for i in range(NTMP)]
        TMPB = [scr_bf(TT) for _ in range(4)]
        bTMPB = [Buf(f"tmpb{i}") for i in range(4)]
        GOP = {}
        bGOP = {}
        for nm in ("AT", "BT", "KT", "RT", "RK", "VB"):
            GOP[nm] = scr_bf(4 * TT).rearrange("p (m t) -> p m t", t=TT)
            bGOP[nm] = [Buf(f"{nm}{m}") for m in range(4)]
        assert _scr_off[0] <= 64 * TT, _scr_off[0]

        def ACT(out, in_, func, reads, writes, bias=None, scale=None):
            kw = {}
            if bias is not None:
                kw["bias"] = bias
            if scale is not None:
                kw["scale"] = scale
            P.op("act", lambda e: e.activation(out=out, in_=in_, func=func, **kw), reads, writes)

        def TTT(out, in0, in1, op, reads, writes, eng="dve"):
            P.op(eng, lambda e: e.tensor_tensor(out=out, in0=in0, in1=in1, op=op), reads, writes)

        def STT(out, in0, scalar, in1, op0, op1, reads, writes):
            P.op("dve", lambda e: e.scalar_tensor_tensor(out=out, in0=in0, scalar=scalar, in1=in1, op0=op0, op1=op1), reads, writes)

        def TS(out, in0, s1, s2, op0, op1, reads, writes):
            if op1 is None:
                P.op("dve", lambda e: e.tensor_scalar(out=out, in0=in0, scalar1=s1, scalar2=None, op0=op0), reads, writes)
            else:
                P.op("dve", lambda e: e.tensor_scalar(out=out, in0=in0, scalar1=s1, scalar2=s2, op0=op0, op1=op1), reads, writes)

        def CP(out, in_, reads, writes, eng="dve"):
            P.op(eng, lambda e: e.tensor_copy(out=out, in_=in_), reads, writes)

        def MM(out, lhsT, rhs, start, stop, reads, writes):
            P.op("pe", lambda e: e.matmul(out, lhsT=lhsT, rhs=rhs, start=start, stop=stop), reads, writes)

        def TR(out, in_, ident, reads, writes):
            P.op("pe", lambda e: e.transpose(out, in_, ident), reads, writes)

        def RECIP(out, in_, reads, writes):
            P.op("dve", lambda e: e.reciprocal(out=out, in_=in_), reads, writes)

        def MEMSET(ap, val, writes, eng="dve"):
            P.op(eng, lambda e: e.memset(ap, val), (), writes)

        def DMA(q, key, pairs, reads, writes):
            fns = [(lambda e, o=o, i=i: e.dma_start(out=o, in_=i)) for (o, i) in pairs]
            P.dma(q, key, fns, reads, writes)

        def vcol(l, c, rows=128):
            return VEC[0:rows, l, c:c + 1]

        ws_ptr = [0]

        def wload(src, n_free, view_fn):
            i = ws_ptr[0] % NWS
            ws_ptr[0] += 1
            dst = view_fn(WS[i][:, 0:n_free])
            DMA("pool", f"w{i}", [(dst, src)], [], [bWS[i]])
            return i, dst

        DMA("sp", "vec", [(VEC[:, :, :], vecd[:, :, :])], [], [bVEC])
        DMA("sp", "cf", [(CF[:, :], cfd[:, :])], [], [bCF])
        DMA("pool", "cb", [(CB[:, :], cbd[:, :])], [], [bCB])
        for l in range(NL):
            TS(VEC[:, l, V_OMM:V_OMM + 28], VEC[:, l, V_MU:V_MU + 28], -1.0, 1.0, ALU.mult, ALU.add, [bVEC], [bVEC])
            TS(VEC[:, l, V_OKA:V_OKA + 8], VEC[:, l, V_KA:V_KA + 8], -1.0, 1.0, ALU.mult, ALU.add, [bVEC], [bVEC])
        MEMSET(CARRY[:, :, :], 0.0, [b for row in bCARRY for b in row])
        MEMSET(CARRYP[:, :, :, :], 0.0, bCARRYP)
        MEMSET(ST[:, :, :, :], 0.0, [b for row in bST for b in row])
        MEMSET(MIX[:, :, :], 0.0, bMIX)

        ONESM = CB[:, CB_ONESM:CB_ONESM + 128]
        IDENTB = CB[:, CB_ID:CB_ID + 128]
        BONES = CB[:, CB_BONES:CB_BONES + 128]
        IDB = CB[:, CB_IDB:CB_IDB + 64]

        def rmsnorm(l, gcol0):
            for c in range(NCH):
                ACT(SQ[:, c % 2, :], X[:, c, :], AF.Square, [bX[c]], [bSQ[c % 2]])
                MM(PS[7][:, 0:TT], ONESM, SQ[:, c % 2, :], c == 0, c == NCH - 1, [bSQ[c % 2], bCB], [bPS[7]])
            ACT(RSTD[:, :], PS[7][:, 0:TT], AF.Sqrt, [bPS[7], bCF], [bRSTD], bias=CF[:, CF_EPSN:CF_EPSN + 1])
            RECIP(RSTD[:, :], RSTD[:, :], [bRSTD], [bRSTD])

        def norm_to_H(l, gcol0):
            rmsnorm(l, gcol0)
            for c in range(NCH):
                STT(H[:, c, :], X[:, c, :], vcol(l, gcol0 + c), RSTD[:, :], ALU.mult, ALU.mult,
                    [bX[c], bRSTD, bVEC], [bH[c]])

        tb_ptr = [0]

        def shift_evac(l, ps_ap, pbuf, rows, cidx, out_ap, out_bufs, extra_reads=()):
            k = tb_ptr[0] % 2
            tb_ptr[0] += 1
            TA = TMP[16 + k]
            bTA = bTMP[16 + k]
            TB = TBUF[:, k, :]
            CP(TB[0:rows, 0:1], CARRY[0:rows, l, cidx:cidx + 1], [bCARRY[l][cidx], bPHASE], [bTB[k]])
            ACT(TA[0:rows, :], ps_ap, AF.Identity, [pbuf, bVEC, bPHASE], [bTA], scale=vcol(l, V_OMM + cidx, rows))
            ACT(TB[0:rows, 1:TT + 1], ps_ap, AF.Identity, [pbuf, bVEC], [bTB[k]], scale=vcol(l, V_MU + cidx, rows))
            TTT(out_ap, TA[0:rows, :], TB[0:rows, 0:TT], ALU.add, [bTA, bTB[k], bPHASE] + list(extra_reads), out_bufs)
            CP(CARRY[0:rows, l, cidx:cidx + 1], TB[0:rows, TT:TT + 1], [bTB[k]], [bCARRY[l][cidx]])

        def big_mm(ps_ap, pbuf, wview_fn, wbuf, rhs_fn, rhs_bufs, nk):
            for k in range(nk):
                MM(ps_ap, wview_fn(k), rhs_fn(k), k == 0, k == nk - 1, [wbuf, rhs_bufs[k]], [pbuf])

        for tile in range(NT):
            t0 = tile * TT
            DMA("sp", "x", [(X[:, :, :], xT.rearrange("c p t -> p c t")[:, :, t0:t0 + TT])], [], bX)
            for l in range(NL):
                first_tile = tile == 0
                norm_to_H(l, V_ATTN)
                if stop_after < 1:
                    continue
                si, wv = wload(wLB[l], 16 * 320, lambda a: a.rearrange("p (k c) -> p k c", c=320))
                P.barrier()
                specs = [(0, 128, 24, 0), (128, 128, 25, 1), (256, 32, 26, 2)]
                if l > 0:
                    specs.append((288, 32, 27, 3))
                for (c0, rows, cidx, bank) in specs:
                    big_mm(PS[bank][0:rows, 0:TT], bPS[bank], lambda k, c0=c0, rows=rows: wv[:, k, c0:c0 + rows], bWS[si],
                           lambda k: H[:, k, :], bH, NCH)
                for (c0, rows, cidx, bank) in specs:
                    zs = TMP[15]
                    shift_evac(l, PS[bank][0:rows, 0:TT], bPS[bank], rows, cidx, zs[0:rows, :], [bTMP[15]])
                    if cidx == 24:
                        ACT(LWA[0:64, :], zs[0:64, :], AF.Tanh, [bTMP[15]], [bLWA])
                        ACT(LWA[64:128, :], zs[64:128, :], AF.Identity, [bTMP[15]], [bLWA])
                    elif cidx == 25:
                        ACT(LG0[:, :], zs[:, :], AF.Sigmoid, [bTMP[15]], [bLG0])
                    elif cidx == 26:
                        ACT(LG1[:, :], zs[0:32, :], AF.Sigmoid, [bTMP[15]], [bLG1])
                    else:
                        ACT(LVR[:, :], zs[0:32, :], AF.Identity, [bTMP[15]], [bLVR])
                if stop_after < 2:
                    continue
                DMA("pool", "smw", [(SMW[:, :], wSM[l])], [], [bSMW])
                WUP = SMW[0:64, 0:1024]
                AUP = SMW[64:128, 0:1024]
                GUP0 = SMW[:, 1024:2048]
                GUP1 = SMW[0:32, 2048:3072]
                VUP = SMW[0:32, 3072:4096]
                POOLW = SMW[:, 4096:6144].rearrange("p (g k d) -> p g k d", g=4, k=2)
                CP(ZP[:, :, 0:16], CARRYP[:, l, :, :], [bCARRYP[l]], bZP)
                for pb in range(2):
                    si, wv = wload(wPB[l, pb], 16 * 512, lambda a: a.rearrange("p (k c) -> p k c", c=512))
                    for j in range(4):
                        c = pb * 4 + j
                        bank = 4 + j
                        big_mm(PS[bank][:, 0:TT], bPS[bank], lambda k, j=j: wv[:, k, j * 128:(j + 1) * 128], bWS[si],
                               lambda k: H[:, k, :], bH, NCH)
                        ACT(ZP[:, c, 16:16 + TT], PS[bank][:, 0:TT], AF.Copy, [bPS[bank]], [bZP[c]])
                CP(CARRYP[:, l, :, :], ZP[:, :, TT:TT + 16], bZP, [bCARRYP[l]])
                if stop_after < 3:
                    continue
                for gp in range(2):
                    for tb in range(NTB):
                        for gg in range(2):
                            g = 2 * gp + gg
                            for kk in range(2):
                                MM(PS[4][:, gg * 256:(gg + 1) * 256], ZP[:, 2 * g + kk, 16 + tb * 128:16 + (tb + 1) * 128],
                                   POOLW[:, g, kk, :], kk == 0, kk == 1, [bZP[2 * g + kk], bSMW], [bPS[4]])
                        for gg in range(2):
                            g = 2 * gp + gg
                            for kk in range(2):
                                MM(PS[5][0:16, gg * 256:(gg + 1) * 256], ZP[:, 2 * g + kk, tb * 128:tb * 128 + 16],
                                   POOLW[:, g, kk, :], kk == 0, kk == 1, [bZP[2 * g + kk], bSMW], [bPS[5]])
                        qi = 0
                        ACT(QC[:, qi, :], PS[4][:, :], AF.Copy, [bPS[4]], [bQC[qi]])
                        CP(QH[:, qi, :], PS[5][0:16, :], [bPS[5]], [bQH[qi]])
                        for dmi in range(4):
                            g = 2 * gp + dmi // 2
                            pmc = CB_PMF if (first_tile and tb == 0) else CB_PMC
                            MM(PS[dmi][:, tb * 128:(tb + 1) * 128], QH[0:16, qi, dmi * 128:(dmi + 1) * 128],
                               CB[0:16, CB_PMH + g * 128:CB_PMH + (g + 1) * 128], True, False, [bQH[qi], bCB], [bPS[dmi]])
                            MM(PS[dmi][:, tb * 128:(tb + 1) * 128], QC[:, qi, dmi * 128:(dmi + 1) * 128],
                               CB[:, pmc + g * 128:pmc + (g + 1) * 128], False, True, [bQC[qi], bCB], [bPS[dmi]])
                    for dmi in range(4):
                        dm = gp * 4 + dmi
                        ACT(MIX[:, dm, :], PS[dmi][:, 0:TT], AF.Identity, [bPS[dmi], bVEC], [bMIX[dm]], scale=vcol(l, V_PSC + dm))
                if stop_after < 4:
                    continue
                CP(RKW[:, :, :], VEC[:, l, V_RK:V_RK + 8].unsqueeze(2).to_broadcast([128, 8, 64]), [bVEC], [bRKW])
                for grp in range(2):
                    for mloc in range(4):
                        m = grp * 4 + mloc
                        rwkv_prep = None
                        MM(PS[4][:, 0:TT], WUP[:, m * 128:(m + 1) * 128], LWA[0:64, :], True, True, [bSMW, bLWA], [bPS[4]])
                        MM(PS[5][:, 0:TT], AUP[:, m * 128:(m + 1) * 128], LWA[64:128, :], True, True, [bSMW, bLWA], [bPS[5]])
                        LD, A32, SG, CS, CSM, IGAM, GAM1, GAM, DREV, GREV = TMP[0:10]
                        bLD, bA32, bSG, bCS, bCSM, bIGAM, bGAM1, bGAM, bDREV, bGREV = bTMP[0:10]
                        ACT(LD, PS[4][:, 0:TT], AF.Sigmoid, [bPS[4], bVEC, bPHASE], [bLD], bias=vcol(l, V_W0 + m))
                        ACT(A32, PS[5][:, 0:TT], AF.Sigmoid, [bPS[5], bVEC, bPHASE], [bA32], bias=vcol(l, V_A0 + m))
                        if l > 0:
                            MM(PS[6][:, 0:TT], VUP[:, m * 128:(m + 1) * 128], LVR[:, :], True, True, [bSMW, bLVR], [bPS[6]])
                            ACT(SG, PS[6][:, 0:TT], AF.Sigmoid, [bPS[6], bVEC, bPHASE], [bSG], bias=vcol(l, V_V0 + m))
                        P.op("dve", lambda e, CS=CS, LD=LD: e.tensor_tensor_scan(out=CS, data0=CF[:, CF_RESET:CF_RESET + TT], data1=LD,
                                                                                 initial=0.0, op0=ALU.mult, op1=ALU.add),
                             [bCF, bLD, bPHASE], [bCS])
                        ACT(IGAM, CS, AF.Exp, [bCS, bPHASE], [bIGAM], scale=C0)
                        ACT(GAM, CS, AF.Exp, [bCS, bPHASE], [bGAM], scale=-C0)
                        TTT(CSM, CS, LD, ALU.subtract, [bCS, bLD, bPHASE], [bCSM])
                        ACT(GAM1, CSM, AF.Exp, [bCSM, bPHASE], [bGAM1], scale=-C0)
                        CS3 = CS.rearrange("p (s t) -> p s t", t=64)
                        TTT(DREV.rearrange("p (s t) -> p s t", t=64), CS3[:, :, 63:64].to_broadcast([128, NSUB, 64]), CS3,
                            ALU.subtract, [bCS, bPHASE], [bDREV])
                        ACT(GREV, DREV, AF.Exp, [bDREV, bPHASE], [bGREV], scale=-C0)
                        GAM3 = GAM.rearrange("p (s t) -> p s t", t=64)
                        TTT(DGE[:, mloc, :, :], CF[:, CF_ID2:CF_ID2 + 64].unsqueeze(1).to_broadcast([128, NSUB, 64]),
                            GAM3[:, :, 63:64].to_broadcast([128, NSUB, 64]), ALU.mult, [bCF, bGAM], [bDGE[mloc]])
                        si, wv = wload(wRKV[l, m], 16 * 384, lambda a: a.rearrange("p (k c) -> p k c", c=384))
                        for j in range(3):
                            big_mm(PS[4 + j][:, 0:TT], bPS[4 + j], lambda k, j=j: wv[:, k, j * 128:(j + 1) * 128], bWS[si],
                                   lambda k: H[:, k, :], bH, NCH)
                        R32, K32, V32, RN, KKN, B32 = TMP[10:16][0], TMP[11], TMP[12], TMP[13], TMP[14], TMP[15]
                        bR32, bK32, bV32, bRN, bKKN, bB32 = bTMP[10], bTMP[11], bTMP[12], bTMP[13], bTMP[14], bTMP[15]
                        shift_evac(l, PS[4][:, 0:TT], bPS[4], 128, 0 + m, R32, [bR32])
                        shift_evac(l, PS[5][:, 0:TT], bPS[5], 128, 8 + m, K32, [bK32])
                        shift_evac(l, PS[6][:, 0:TT], bPS[6], 128, 16 + m, V32, [bV32])
                        TTT(GOP["RT"][:, mloc, :], R32, GAM, ALU.mult, [bR32, bGAM, bPHASE], [bGOP["RT"][mloc]])
                        KSQ = TMPB[0]
                        ACT(KSQ, K32, AF.Square, [bK32, bVEC, bPHASE], [bTMPB[0]], scale=vcol(l, V_KK + m))
                        MM(PS[7][:, 0:TT], BONES, KSQ, True, True, [bCB, bTMPB[0]], [bPS[7]])
                        ACT(RN, PS[7][:, 0:TT], AF.Sqrt, [bPS[7], bPHASE], [bRN])
                        TS(RN, RN, 1e-12, None, ALU.max, None, [bRN], [bRN])
                        RECIP(RN, RN, [bRN], [bRN])
                        STT(KKN, K32, vcol(l, V_KK + m), RN, ALU.mult, ALU.mult, [bK32, bRN, bVEC, bPHASE], [bKKN])
                        STT(GOP["AT"][:, mloc, :], KKN, -1.0, GAM1, ALU.mult, ALU.mult, [bKKN, bGAM1, bPHASE], [bGOP["AT"][mloc]])
                        TTT(B32, KKN, A32, ALU.mult, [bKKN, bA32, bPHASE], [bB32])
                        TTT(GOP["BT"][:, mloc, :], B32, IGAM, ALU.mult, [bB32, bIGAM, bPHASE], [bGOP["BT"][mloc]])
                        BH = TMPB[1]
                        TTT(BH, B32, GREV, ALU.mult, [bB32, bGREV, bPHASE], [bTMPB[1]])
                        T1 = RN
                        TS(T1, A32, vcol(l, V_KA + m), vcol(l, V_OKA + m), ALU.mult, ALU.add, [bA32, bVEC, bPHASE], [bRN])
                        KP = KKN
                        TTT(KP, K32, T1, ALU.mult, [bK32, bRN, bPHASE], [bKKN])
                        TTT(GOP["KT"][:, mloc, :], KP, IGAM, ALU.mult, [bKKN, bIGAM, bPHASE], [bGOP["KT"][mloc]])
                        KH = TMPB[2]
                        TTT(KH, KP, GREV, ALU.mult, [bKKN, bGREV, bPHASE], [bTMPB[2]])
                        TTT(GOP["RK"][:, mloc, :], R32, KP, ALU.mult, [bR32, bKKN, bPHASE], [bGOP["RK"][mloc]])
                        if l == 0:
                            CP(VF[:, m, :], V32, [bV32, bPHASE], [bVF[m]])
                            CP(GOP["VB"][:, mloc, :], V32, [bV32, bPHASE], [bGOP["VB"][mloc]])
                        else:
                            TTT(B32, VF[:, m, :], V32, ALU.subtract, [bVF[m], bV32, bPHASE], [bB32])
                            TTT(B32, B32, SG, ALU.mult, [bB32, bSG, bPHASE], [bB32])
                            TTT(GOP["VB"][:, mloc, :], V32, B32, ALU.add, [bV32, bB32, bPHASE], [bGOP["VB"][mloc]])
                        srcs = [(GOP["AT"][:, mloc, :], bGOP["AT"][mloc]), (BH, bTMPB[1]), (KH, bTMPB[2]),
                                (GOP["VB"][:, mloc, :], bGOP["VB"][mloc])]
                        for sp in range(NSUB // 2):
                            bank = sp % 2
                            PSB = PS[bank][:, :].bitcast(BF16)
                            for s2 in range(2):
                                s = sp * 2 + s2
                                for qi, (src, sbf) in enumerate(srcs):
                                    TR(PSB[0:64, (s2 * 4 + qi) * 128:(s2 * 4 + qi + 1) * 128], src[:, s * 64:(s + 1) * 64], IDENTB,
                                       [sbf, bCB, bPHASE], [bPS[bank]])
                            CP(TMALL[:, mloc, sp * 2:sp * 2 + 2, :, :].rearrange("p s q c -> p (s q c)"), PSB[0:64, :],
                               [bPS[bank]], [bTM[mloc][sp * 2], bTM[mloc][sp * 2 + 1]])
                    if stop_after < 5:
                        continue
                    MU_S = CF[0:64, CF_MU_S:CF_MU_S + 64].unsqueeze(1).to_broadcast([64, 8, 64])
                    MU_I = CF[0:64, CF_MU_I:CF_MU_I + 64].unsqueeze(1).to_broadcast([64, 8, 64])
                    ML_S = CF[0:64, CF_ML_S:CF_ML_S + 64].unsqueeze(1).to_broadcast([64, 8, 64])
                    ID64 = CF[0:64, CF_ID64:CF_ID64 + 64].unsqueeze(1).to_broadcast([64, 8, 64])
                    bk = [0]
                    bko = [0]

                    def nbank():
                        b = bk[0] % 6
                        bk[0] += 1
                        return b

                    def nbank_o():
                        b = 6 + bko[0] % 2
                        bko[0] += 1
                        return b

                    def pview(b):
                        return PS[b][0:64, :].rearrange("p (h t) -> p h t", t=64)

                    def SELJ(j):
                        return CB[:, CB_SEL + 64 * j:CB_SEL + 64 * (j + 1)]

                    for s in range(NSUB):
                        tc0 = s * 64

                        def fm(nm, hi):
                            j, ml = hi // 4, hi % 4
                            return GOP[nm][64 * j:64 * j + 64, ml, tc0:tc0 + 64], bGOP[nm][ml]

                        def tm(q, hi):
                            j, ml = hi // 4, hi % 4
                            return TMALL[:, ml, s, q, 64 * j:64 * j + 64], bTM[ml][s]

                        (cM, cN, cARB, cAAK, cARK, cP0, cP1, cM2, cN2, cX, cQQ, cPP, cRP, cGG) = range(14)

                        def amat(lname, rname, ci, mask):
                            be = nbank()
                            bo = nbank_o()
                            for hi in range(8):
                                la, lb = fm(lname, hi)
                                ra, rb = fm(rname, hi)
                                b = be if hi < 4 else bo
                                MM(pview(b)[:, hi % 4, :], la, ra, True, True, [lb, rb, bPHASE], [bPS[b]])
                            TTT(CH[:, ci, 0:4, :], pview(be)[:, 0:4, :], mask[:, 0:4, :], ALU.mult, [bPS[be], bCF], [bCH[ci]])
                            TTT(CH[:, ci, 4:8, :], pview(bo)[:, 0:4, :], mask[:, 0:4, :], ALU.mult, [bPS[bo], bCF], [bCH[ci]])

                        amat("BT", "AT", cM, MU_S)
                        amat("AT", "BT", cN, ML_S)
                        amat("BT", "RT", cARB, MU_I)
                        amat("KT", "AT", cAAK, MU_S)
                        amat("KT", "RT", cARK, MU_I)
                        TTT(CH[:, cP0, :, :], CH[:, cM, :, :], ID64, ALU.add, [bCH[cM], bCF], [bCH[cP0]])
                        curM, curN, curP = cM, cN, cP0
                        altM, altN, altP = cM2, cN2, cP1
                        for lev in range(5):
                            bN = nbank()
                            for hi in range(8):
                                MM(pview(bN)[:, hi, :], CH[:, curM, hi, :], CH[:, curN, hi, :], True, True,
                                   [bCH[curM], bCH[curN]], [bPS[bN]])
                            ACT(CH[:, altN, :, :], pview(bN), AF.Copy, [bPS[bN]], [bCH[altN]])
                            if lev < 4:
                                bM = nbank()
                                for hi in range(8):
                                    MM(pview(bM)[:, hi, :], CH[:, curN, hi, :], CH[:, curM, hi, :], True, True,
                                       [bCH[curM], bCH[curN]], [bPS[bM]])
                                ACT(CH[:, altM, :, :], pview(bM), AF.Copy, [bPS[bM]], [bCH[altM]])
                            bP = nbank()
                            for hi in range(8):
                                MM(pview(bP)[:, hi, :], CH[:, altN, hi, :], CH[:, curP, hi, :], True, True,
                                   [bCH[altN], bCH[curP]], [bPS[bP]])
                            TTT(CH[:, altP, :, :], CH[:, curP, :, :], pview(bP), ALU.add, [bCH[curP], bPS[bP]], [bCH[altP]])
                            curM, altM = altM, curM
                            curN, altN = altN, curN
                            curP, altP = altP, curP
                        cTt = curP
                        b = nbank()
                        for hi in range(8):
                            va, vb = tm(3, hi)
                            MM(pview(b)[:, hi, :], CH[:, cAAK, hi, :], va, True, True, [bCH[cAAK], vb], [bPS[b]])
                        ACT(CH[:, cX, :, :], pview(b), AF.Copy, [bPS[b]], [bCH[cX]])
                        b = nbank()
                        for hi in range(8):
                            MM(pview(b)[:, hi, :], CH[:, cTt, hi, :], CH[:, cX, hi, :], True, True, [bCH[cTt], bCH[cX]], [bPS[b]])
                        ACT(CH[:, cQQ, :, :], pview(b), AF.Copy, [bPS[b]], [bCH[cQQ]])
                        b = nbank()
                        for hi in range(8):
                            aa, ab = tm(0, hi)
                            MM(pview(b)[:, hi, :], CH[:, cTt, hi, :], aa, True, True, [bCH[cTt], ab], [bPS[b]])
                        CP(CH[:, cPP, :, :], pview(b), [bPS[b]], [bCH[cPP]])
                        b = nbank()
                        for hi in range(8):
                            j, ml = hi // 4, hi % 4
                            MM(pview(b)[:, hi, :], SELJ(j), GOP["RT"][:, ml, tc0:tc0 + 64], True, False, [bCB, bGOP["RT"][ml]], [bPS[b]])
                            MM(pview(b)[:, hi, :], CH[:, cPP, hi, :], CH[:, cARB, hi, :], False, True, [bCH[cPP], bCH[cARB]], [bPS[b]])
                        ACT(CH[:, cRP, :, :], pview(b), AF.Copy, [bPS[b]], [bCH[cRP]])
                        b = nbank()
                        for hi in range(8):
                            j, ml = hi // 4, hi % 4
                            ba, bb = tm(1, hi)
                            MM(pview(b)[:, hi, :], SELJ(j), DGE[:, ml, s, :], True, False, [bCB, bDGE[ml]], [bPS[b]])
                            MM(pview(b)[:, hi, :], CH[:, cPP, hi, :], ba, False, True, [bCH[cPP], bb], [bPS[b]])
                        CP(CH[:, cGG, :, :], pview(b), [bPS[b]], [bCH[cGG]])
                        b = nbank()
                        stb = [bST[l][grp * 8 + hi] for hi in range(8)]
                        for hi in range(8):
                            h = grp * 8 + hi
                            va, vb = tm(3, hi)
                            MM(pview(b)[:, hi, :], CH[:, cQQ, hi, :], CH[:, cARB, hi, :], True, False, [bCH[cQQ], bCH[cARB]], [bPS[b]])
                            MM(pview(b)[:, hi, :], va, CH[:, cARK, hi, :], False, False, [vb, bCH[cARK]], [bPS[b]])
                            MM(pview(b)[:, hi, :], ST[:, l, h, :], CH[:, cRP, hi, :], False, True, [stb[hi], bCH[cRP]], [bPS[b]])
                        ACT(Y32[:, :, tc0:tc0 + 64], pview(b), AF.Copy, [bPS[b]], bY32)
                        b = nbank()
                        for hi in range(8):
                            h = grp * 8 + hi
                            ba, bb = tm(1, hi)
                            ka, kb = tm(2, hi)
                            va, vb = tm(3, hi)
                            MM(pview(b)[:, hi, :], ba, CH[:, cQQ, hi, :], True, False, [bb, bCH[cQQ]], [bPS[b]])
                            MM(pview(b)[:, hi, :], ka, va, False, False, [kb, vb], [bPS[b]])
                            MM(pview(b)[:, hi, :], CH[:, cGG, hi, :], ST[:, l, h, :], False, True, [bCH[cGG], stb[hi]], [bPS[b]])
                        CP(ST[:, l, grp * 8:grp * 8 + 8, :], pview(b), [bPS[b]], stb)
                    if stop_after < 6:
                        continue
                    for hh in range(8):
                        j, ml = hh // 4, hh % 4
                        h = grp * 8 + 2 * ml + j
                        bc_, bv_ = (2, 3) if j == 0 else (6, 7)
                        yh = Y32[:, hh, :]
                        YSQ, MUt, T1o, RS, Dd, CBt = [OT[:, i, :] for i in range(6)]
                        ACT(YSQ, yh, AF.Square, [bY32[hh]], [bOT[0]])
                        O64 = CF[0:64, CF_ONES64:CF_ONES64 + 64]
                        MM(PS[0][0:64, 0:TT], O64, yh, True, True, [bCF, bY32[hh]], [bPS[0]])
                        MM(PS[1][0:64, 0:TT], O64, YSQ, True, True, [bCF, bOT[0]], [bPS[1]])
                        ACT(MUt, PS[0][0:64, 0:TT], AF.Copy, [bPS[0]], [bOT[1]])
                        TTT(T1o, MUt, MUt, ALU.mult, [bOT[1]], [bOT[2]])
                        TTT(T1o, PS[1][0:64, 0:TT], T1o, ALU.subtract, [bPS[1], bOT[2]], [bOT[2]])
                        ACT(RS, T1o, AF.Sqrt, [bOT[2], bCF], [bOT[3]], bias=CF[0:64, CF_EPSG:CF_EPSG + 1])
                        RECIP(RS, RS, [bOT[3]], [bOT[3]])
                        TTT(Dd, yh, MUt, ALU.subtract, [bY32[hh], bOT[1]], [bOT[4]])
                        TTT(Dd, Dd, RS, ALU.mult, [bOT[4], bOT[3]], [bOT[4]])
                        TS(Dd, Dd, VEC[0:64, l, V_GNG + h:V_GNG + h + 1], VEC[0:64, l, V_GNB + h:V_GNB + h + 1], ALU.mult, ALU.add,
                           [bOT[4], bVEC], [bOT[4]])
                        MM(PS[bc_][0:64, 0:TT], RKW[64 * j:64 * j + 64, grp * 4 + ml, :], GOP["RK"][64 * j:64 * j + 64, ml, :], True, True,
                           [bRKW, bGOP["RK"][ml], bPHASE], [bPS[bc_]])
                        MM(PS[bv_][0:64, 0:TT], SELJ(j), GOP["VB"][:, ml, :], True, True,
                           [bCB, bGOP["VB"][ml], bPHASE], [bPS[bv_]])
                        ACT(CBt, PS[bc_][0:64, 0:TT], AF.Copy, [bPS[bc_]], [bOT[5]])
                        TTT(CBt, CBt, PS[bv_][0:64, 0:TT], ALU.mult, [bOT[5], bPS[bv_]], [bOT[5]])
                        TTT(Dd, Dd, CBt, ALU.add, [bOT[4], bOT[5]], [bOT[4]])
                        gb = 4 + hh % 2
                        MM(PS[gb][0:64, 0:TT], GUP0[:, h * 64:(h + 1) * 64], LG0[:, :], True, False, [bSMW, bLG0], [bPS[gb]])
                        MM(PS[gb][0:64, 0:TT], GUP1[:, h * 64:(h + 1) * 64], LG1[:, :], False, True, [bSMW, bLG1], [bPS[gb]])
                        TTT(MIX[0:64, 8 + h, :], Dd, PS[gb][0:64, 0:TT], ALU.mult, [bOT[4], bPS[gb]], [bMIX[8 + h]])
                if stop_after < 7:
                    continue
                for cb in range(8):
                    si, wv = wload(wWO[l, cb], 24 * 256, lambda a: a.rearrange("p (k c) -> p k c", c=256))
                    for oc2 in range(2):
                        oc = cb * 2 + oc2
                        bank = oc % 4
                        for k in range(8):
                            MM(PS[bank][:, 0:TT], wv[:, k, oc2 * 128:(oc2 + 1) * 128], MIX[:, k, :], k == 0, False,
                               [bWS[si], bMIX[k]], [bPS[bank]])
                        for h in range(16):
                            MM(PS[bank][:, 0:TT], wv[0:64, 8 + h, oc2 * 128:(oc2 + 1) * 128], MIX[0:64, 8 + h, :], False, h == 15,
                               [bWS[si], bMIX[8 + h]], [bPS[bank]])
                        TTT(X[:, oc, :], X[:, oc, :], PS[bank][:, 0:TT], ALU.add, [bX[oc], bPS[bank]], [bX[oc]])
                if stop_after < 8:
                    continue
                norm_to_H(l, V_MLP)
                P.barrier()
                first_hid = False
                for ub in range(16):
                    si, wv = wload(wUP[l, ub], 16 * 512, lambda a: a.rearrange("p (k c) -> p k c", c=512))
                    for hc4 in range(4):
                        hc = ub * 4 + hc4
                        bank = 4 + hc % 4
                        big_mm(PS[bank][:, 0:TT], bPS[bank], lambda k, hc4=hc4: wv[:, k, hc4 * 128:(hc4 + 1) * 128], bWS[si],
                               lambda k: H[:, k, :], bH, NCH)
                        k2 = hc % 2
                        ACT(SQ[:, k2, :], PS[bank][:, 0:TT], AF.Relu, [bPS[bank]], [bSQ[k2]])
                        if first_hid:
                            TTT(HID[:, hc, :], SQ[:, k2, :], SQ[:, k2, :], ALU.mult, [bSQ[k2]], [bHID[hc], bPHASE])
                            first_hid = False
                        else:
                            TTT(HID[:, hc, :], SQ[:, k2, :], SQ[:, k2, :], ALU.mult, [bSQ[k2], bPHASE], [bHID[hc]])
                for cb in range(4):
                    for kg in range(4):
                        si, wv = wload(wDN[l, cb, kg], 16 * 512, lambda a: a.rearrange("p (k c) -> p k c", c=512))
                        for oc4 in range(4):
                            for k in range(16):
                                MM(PS[oc4][:, 0:TT], wv[:, k, oc4 * 128:(oc4 + 1) * 128], HID[:, kg * 16 + k, :],
                                   kg == 0 and k == 0, kg == 3 and k == 15, [bWS[si], bHID[kg * 16 + k], bPHASE], [bPS[oc4]])
                    for oc4 in range(4):
                        oc = cb * 4 + oc4
                        TTT(X[:, oc, :], X[:, oc, :], PS[oc4][:, 0:TT], ALU.add, [bX[oc], bPS[oc4]], [bX[oc]])
                if stop_after < 9:
                    continue
                norm_to_H(l, V_PLE)
                DMA("pool", "ptb", [(PTB[:, :, :], pT[l].rearrange("k p t -> p k t")[:, :, t0:t0 + TT])], [], [bPTB])
                DMA("pool", "pjw", [(PJW[:, :, :], wPJ[l])], [], [bPJW])
                pj = PJW
                for cb in range(4):
                    si, wv = wload(wGT[l, cb], 16 * 512, lambda a: a.rearrange("p (k c) -> p k c", c=512))
                    for oc4 in range(4):
                        oc = cb * 4 + oc4
                        bank = 4 + oc4
                        big_mm(PS[bank][:, 0:TT], bPS[bank], lambda k, oc4=oc4: wv[:, k, oc4 * 128:(oc4 + 1) * 128], bWS[si],
                               lambda k: H[:, k, :], bH, NCH)
                        for k in range(2):
                            MM(PS[oc4][:, 0:TT], pj[:, k, oc * 128:(oc + 1) * 128], PTB[:, k, :], k == 0, k == 1,
                               [bPJW, bPTB], [bPS[oc4]])
                        gt = TBUF[:, oc4 % 2, 0:TT]
                        ACT(gt, PS[bank][:, 0:TT], AF.Sigmoid, [bPS[bank]], [bTB[oc4 % 2]])
                        TTT(gt, gt, PS[oc4][:, 0:TT], ALU.mult, [bTB[oc4 % 2], bPS[oc4]], [bTB[oc4 % 2]])
                        TTT(X[:, oc, :], X[:, oc, :], gt, ALU.add, [bX[oc], bTB[oc4 % 2]], [bX[oc]])
            rmsnorm(0, V_FIN)
            for c in range(NCH):
                k = c % 2
                STT(TBUF[:, k, 0:TT], X[:, c, :], vcol(0, V_FIN + c), RSTD[:, :], ALU.mult, ALU.mult, [bX[c], bRSTD, bVEC], [bTB[k]])
                DMA("sp", f"o{k}", [(oT[c, :, t0:t0 + TT], TBUF[:, k, 0:TT])], [bTB[k]], [])
        P.wait_all("sp", bTB)
        P.emit()
    return nc


def _blk(w, nk):
    return np.ascontiguousarray(w.reshape(nk, 128, w.shape[1]).transpose(1, 0, 2))


def make_consts():
    cb = np.zeros((128, NCB), np.float32)
    cb[:, CB_ID:CB_ID + 128] = np.eye(128)
    cb[:, CB_ONESM:CB_ONESM + 128] = 1.0 / 2048
    blk = np.arange(128) // 64
    cb[:, CB_BONES:CB_BONES + 128] = (blk[:, None] == blk[None, :]).astype(np.float32)
    cb[:, CB_IDB:CB_IDB + 64] = (np.arange(128)[:, None] % 64 == np.arange(64)[None, :]).astype(np.float32)
    for j in range(2):
        cb[:, CB_SEL + j * 64:CB_SEL + (j + 1) * 64] = (np.arange(128)[:, None] == 64 * j + np.arange(64)[None, :]).astype(np.float32)
    t = np.arange(128)
    for g, w in enumerate((2, 4, 8, 16)):
        dlt = t[None, :] - t[:, None]
        band = ((dlt >= 0) & (dlt < w)).astype(np.float32)
        cb[:, CB_PMC + g * 128:CB_PMC + (g + 1) * 128] = band / w - np.eye(128)
        cnt = np.minimum(t + 1, w).astype(np.float32)
        cb[:, CB_PMF + g * 128:CB_PMF + (g + 1) * 128] = band / cnt[None, :] - np.eye(128)
        i = np.arange(16)
        dh = t[None, :] + 16 - i[:, None]
        cb[0:16, CB_PMH + g * 128:CB_PMH + (g + 1) * 128] = ((dh < w)).astype(np.float32) / w
    cf = np.zeros((128, NCF), np.float32)
    r = np.arange(64)
    cf[0:64, CF_MU_S:CF_MU_S + 64] = (r[None, :] > r[:, None])
    cf[0:64, CF_MU_I:CF_MU_I + 64] = (r[None, :] >= r[:, None])
    cf[0:64, CF_ML_S:CF_ML_S + 64] = (r[None, :] < r[:, None])
    cf[0:64, CF_ID64:CF_ID64 + 64] = np.eye(64)
    cf[:, CF_RESET:CF_RESET + TT] = (np.arange(TT) % 64 != 0).astype(np.float32)[None, :]
    cf[:, CF_ID2:CF_ID2 + 64] = (np.arange(128)[:, None] % 64 == np.arange(64)[None, :])
    cf[0:64, CF_ONES64:CF_ONES64 + 64] = 1.0 / 64
    cf[:, CF_EPSN] = NORM_EPS
    cf[:, CF_EPSG] = GN_EPS
    return cb, cf


def prep_weights(inp, NL=DEPTH):
    L = NL
    f = np.float32
    w_in = inp["w_in"]
    out = {}
    wLB = np.zeros((L, 128, 16, 320), f)
    wPB = np.zeros((L, 2, 128, 16, 512), f)
    wRKV = np.zeros((L, 8, 128, 16, 384), f)
    wSM = np.zeros((L, 128, 6144), f)
    wWO = np.zeros((L, 8, 128, 24, 256), f)
    wUP = np.zeros((L, 16, 128, 16, 512), f)
    wDN = np.zeros((L, 4, 4, 128, 16, 512), f)
    wGT = np.zeros((L, 4, 128, 16, 512), f)
    wPJ = np.zeros((L, 128, 2, 2048), f)
    vec = np.zeros((128, DEPTH, NV), f)
    for l in range(L):
        W = w_in[l]
        lb = W[:, 4096:4384]
        if l > 0:
            lb = np.concatenate([lb, inp["w_vres_dn"][l - 1]], axis=1)
        else:
            lb = np.concatenate([lb, np.zeros((D, 32), f)], axis=1)
        wLB[l] = _blk(lb, 16)
        for pb in range(2):
            wPB[l, pb] = _blk(W[:, pb * 512:(pb + 1) * 512], 16)
        for m in range(8):
            cols = np.concatenate([W[:, 1024 + j * 1024 + m * 128:1024 + j * 1024 + (m + 1) * 128] for j in range(3)], axis=1)
            wRKV[l, m] = _blk(cols, 16)
        wSM[l, 0:64, 0:1024] = inp["w_up"][l]
        wSM[l, 64:128, 0:1024] = inp["a_up"][l]
        wSM[l, :, 1024:2048] = inp["g_up"][l][0:128]
        wSM[l, 0:32, 2048:3072] = inp["g_up"][l][128:160]
        if l > 0:
            wSM[l, 0:32, 3072:4096] = inp["v_up"][l - 1]
        pw = inp["pool_w"][l]
        wSM[l, :, 4096:6144] = pw.reshape(4, 2, 128, 256).transpose(2, 0, 1, 3).reshape(128, 2048)
        wo = inp["w_out"][l]
        for cb in range(8):
            sl = wo[:, cb * 256:(cb + 1) * 256]
            wWO[l, cb, :, 0:8, :] = sl[0:1024].reshape(8, 128, 256).transpose(1, 0, 2)
            wWO[l, cb, 0:64, 8:24, :] = sl[1024:2048].reshape(16, 64, 256).transpose(1, 0, 2)
        wu = inp["w_ffn_up"][l]
        for ub in range(16):
            wUP[l, ub] = _blk(wu[:, ub * 512:(ub + 1) * 512], 16)
        wd = inp["w_ffn_down"][l]
        for cb in range(4):
            for kg in range(4):
                wDN[l, cb, kg] = _blk(wd[kg * 2048:(kg + 1) * 2048, cb * 512:(cb + 1) * 512], 16)
        wg = inp["w_ple_gate"][l]
        for cb in range(4):
            wGT[l, cb] = _blk(wg[:, cb * 512:(cb + 1) * 512], 16)
        wPJ[l] = _blk(inp["w_ple_proj"][l], 2)
        vec[:, l, V_ATTN:V_ATTN + 16] = inp["attn_norm"][l].reshape(16, 128).T
        vec[:, l, V_MLP:V_MLP + 16] = inp["mlp_norm"][l].reshape(16, 128).T
        vec[:, l, V_PLE:V_PLE + 16] = inp["ple_norm"][l].reshape(16, 128).T
        vec[:, l, V_FIN:V_FIN + 16] = inp["final_norm"].reshape(16, 128).T
        mu = inp["mu_shift"][l]
        vec[:, l, V_MU:V_MU + 24] = mu[0:3072].reshape(24, 128).T
        vec[:, l, V_MU + 24] = mu[3072:3200]
        vec[:, l, V_MU + 25] = mu[3200:3328]
        vec[0:32, l, V_MU + 26] = mu[3328:3360]
        if l > 0:
            vec[0:32, l, V_MU + 27] = inp["mu_vres"][l - 1]
        vec[:, l, V_PSC:V_PSC + 8] = inp["pool_scale"][l].reshape(8, 128).T
        vec[:, l, V_W0:V_W0 + 8] = inp["w0"][l].reshape(8, 128).T
        vec[:, l, V_A0:V_A0 + 8] = inp["a0"][l].reshape(8, 128).T
        if l > 0:
            vec[:, l, V_V0:V_V0 + 8] = inp["v0"][l - 1].reshape(8, 128).T
        vec[:, l, V_KK:V_KK + 8] = inp["k_k"][l].reshape(8, 128).T
        vec[:, l, V_KA:V_KA + 8] = inp["k_a"][l].reshape(8, 128).T
        vec[:, l, V_RK:V_RK + 8] = inp["r_k"][l].reshape(8, 128).T
        vec[0:64, l, V_GNG:V_GNG + 16] = inp["gn_g"][l].reshape(16, 64).T
        vec[0:64, l, V_GNB:V_GNB + 16] = inp["gn_b"][l].reshape(16, 64).T
    return dict(wLB=wLB, wPB=wPB, wRKV=wRKV, wSM=wSM, wWO=wWO, wUP=wUP, wDN=wDN, wGT=wGT, wPJ=wPJ, vec=vec)


def run_device(inputs, T_run=SEQ, NL=DEPTH, n_cores=8, stop_after=99):
    inp = {k: np.asarray(v, dtype=np.float32) for k, v in inputs.items()}
    wd = prep_weights(inp, NL)
    cb, cf = make_consts()
    NT = T_run // TT
    nc = build_program(NT, NL, T_run, stop_after)
    in_maps = []
    for core in range(n_cores):
        b = (core // 2) % BATCH
        xT = np.ascontiguousarray(inp["x"][b, :T_run].T).reshape(NCH, 128, T_run)
        pT = np.ascontiguousarray(inp["p"][:, b, :T_run].transpose(0, 2, 1)).reshape(DEPTH, 2, 128, T_run)
        m = dict(xT=xT, pT=pT, cbf=cb, cf32=cf)
        m.update(wd)
        in_maps.append(m)
    res = run_bass_kernel_spmd(nc, in_maps, core_ids=list(range(n_cores)))
    outs = []
    for b in range(BATCH):
        o = res.results[2 * b]["oT"] if n_cores == 8 else res.results[min(2 * b, n_cores - 1)]["oT"]
        outs.append(np.asarray(o).reshape(D, T_run).T)
    return np.stack(outs, axis=0).astype(np.float32)


def kernel(**inputs):
    return run_device(inputs)
```

```python
import numpy as np
import concourse.bass as bass
import concourse.mybir as mybir
from concourse.bass_utils import run_bass_kernel_spmd

F32 = mybir.dt.float32
BF16 = mybir.dt.bfloat16
AF = mybir.ActivationFunctionType
ALU = mybir.AluOpType

D = 2048
SEQ = 4096
BATCH = 4
DEPTH = 4
TT = 256
NSUB = TT // 64
NTB = TT // 128
NCH = 16
C0 = float(np.exp(-0.5))
NORM_EPS = 1e-6
GN_EPS = 64e-5
WSLOT = 8192
NWS = 2

V_ATTN, V_MLP, V_PLE, V_FIN = 0, 16, 32, 48
V_MU = 64
V_OMM = 92
V_PSC = 120
V_W0, V_A0, V_V0, V_KK, V_KA, V_OKA, V_RK = 128, 136, 144, 152, 160, 168, 176
V_GNG, V_GNB = 184, 200
NV = 216

CB_ID = 0
CB_ONESM = 128
CB_BONES = 256
CB_IDB = 384
CB_PMC = 448
CB_PMH = 960
CB_PMF = 1472
CB_SEL = 1984
NCB = 2112
CF_MU_S = 0
CF_MU_I = 64
CF_ML_S = 128
CF_ID64 = 192
CF_RESET = 256
CF_ID2 = CF_RESET + TT
CF_ONES64 = CF_ID2 + 64
CF_EPSN = CF_ONES64 + 64
CF_EPSG = CF_EPSN + 1
NCF = CF_EPSG + 1


class Buf:
    __slots__ = ("w", "r", "name")

    def __init__(self, name=""):
        self.w = None
        self.r = {}
        self.name = name


class Prog:
    ENG = ["pe", "act", "dve", "pool", "sp"]

    def __init__(self, nc):
        self.nc = nc
        self.q = {e: [] for e in self.ENG}
        self.cnt = {e: 0 for e in self.ENG}
        self.sem = {e: nc.alloc_semaphore("s_" + e) for e in self.ENG}
        self.waited = {e: {} for e in self.ENG}
        self.dsem = {}
        self.dcnt = {}

    def _semobj(self, key):
        return self.sem[key] if key in self.sem else self.dsem[key]

    def _waits(self, eng, reads, writes):
        deps = {}
        for b in reads:
            if b.w is not None:
                k, v = b.w
                if deps.get(k, 0) < v:
                    deps[k] = v
        for b in writes:
            if b.w is not None:
                k, v = b.w
                if deps.get(k, 0) < v:
                    deps[k] = v
            for k, v in b.r.items():
                if deps.get(k, 0) < v:
                    deps[k] = v
        waits = []
        wd = self.waited[eng]
        for k, v in deps.items():
            if k == eng and eng == "pe":
                continue
            if wd.get(k, 0) < v:
                waits.append((k, v))
                wd[k] = v
        return waits

    def op(self, eng, fn, reads=(), writes=()):
        waits = self._waits(eng, reads, writes)
        self.cnt[eng] += 1
        c = self.cnt[eng]
        self.q[eng].append((waits, fn, None))
        for b in reads:
            if b.r.get(eng, 0) < c:
                b.r[eng] = c
        for b in writes:
            b.w = (eng, c)
            b.r = {}

    def dma(self, q, key, fns, reads=(), writes=()):
        if key not in self.dsem:
            self.dsem[key] = self.nc.alloc_semaphore("d_" + key)
            self.dcnt[key] = 0
        waits = self._waits(q, reads, writes)
        self.dcnt[key] += 16 * len(fns)
        v = self.dcnt[key]
        for i, fn in enumerate(fns):
            self.q[q].append((waits if i == 0 else [], fn, key))
        for b in reads:
            if b.r.get(key, 0) < v:
                b.r[key] = v
        for b in writes:
            b.w = (key, v)
            b.r = {}

    def barrier(self, engs=("pe", "act", "dve")):
        snap = {e: self.cnt[e] for e in engs}
        for e in engs:
            waits = []
            for e2 in engs:
                if e2 == e or snap[e2] == 0:
                    continue
                if self.waited[e].get(e2, 0) < snap[e2]:
                    waits.append((e2, snap[e2]))
                    self.waited[e][e2] = snap[e2]
            if waits:
                self.q[e].append((waits, None, None))

    def wait_all(self, eng, bufs):
        waits = self._waits(eng, bufs, bufs)
        self.q[eng].append((waits, None, None))

    def emit(self):
        nc = self.nc
        P = self

        def replay(name, e):
            for waits, fn, dkey in P.q[name]:
                for k, v in waits:
                    e.wait_ge(P._semobj(k), v)
                if fn is None:
                    continue
                ins = fn(e)
                if dkey is not None:
                    ins.then_inc(P.dsem[dkey], 16)
                else:
                    ins.then_inc(P.sem[name], 1)

        with nc.Block() as block:

            @block.tensor
            def _(e):
                replay("pe", e)

            @block.scalar
            def _(e):
                replay("act", e)

            @block.vector
            def _(e):
                replay("dve", e)

            @block.gpsimd
            def _(e):
                replay("pool", e)

            @block.sync
            def _(e):
                replay("sp", e)


def build_program(NT, NL, T_in, stop_after=99):
    nc = bass.Bass("TRN2", target_bir_lowering=False)
    dt = nc.dram_tensor
    xT = dt("xT", [NCH, 128, T_in], F32, kind="ExternalInput").ap()
    pT = dt("pT", [DEPTH, 2, 128, T_in], F32, kind="ExternalInput").ap()
    oT = dt("oT", [NCH, 128, T_in], F32, kind="ExternalOutput").ap()
    vecd = dt("vec", [128, DEPTH, NV], F32, kind="ExternalInput").ap()
    cbd = dt("cbf", [128, NCB], F32, kind="ExternalInput").ap()
    cfd = dt("cf32", [128, NCF], F32, kind="ExternalInput").ap()
    wLB = dt("wLB", [NL, 128, 16, 320], F32, kind="ExternalInput").ap()
    wPB = dt("wPB", [NL, 2, 128, 16, 512], F32, kind="ExternalInput").ap()
    wRKV = dt("wRKV", [NL, 8, 128, 16, 384], F32, kind="ExternalInput").ap()
    wSM = dt("wSM", [NL, 128, 6144], F32, kind="ExternalInput").ap()
    wWO = dt("wWO", [NL, 8, 128, 24, 256], F32, kind="ExternalInput").ap()
    wUP = dt("wUP", [NL, 16, 128, 16, 512], F32, kind="ExternalInput").ap()
    wDN = dt("wDN", [NL, 4, 4, 128, 16, 512], F32, kind="ExternalInput").ap()
    wGT = dt("wGT", [NL, 4, 128, 16, 512], F32, kind="ExternalInput").ap()
    wPJ = dt("wPJ", [NL, 128, 2, 2048], F32, kind="ExternalInput").ap()

    cLB = dt("cLB", [NL, 128, 16, 320], BF16, kind="Internal").ap()
    cPB = dt("cPB", [NL, 2, 128, 16, 512], BF16, kind="Internal").ap()
    cRKV = dt("cRKV", [NL, 8, 128, 16, 384], BF16, kind="Internal").ap()
    cWO = dt("cWO", [NL, 8, 128, 24, 256], BF16, kind="Internal").ap()
    cUP = dt("cUP", [NL, 16, 128, 16, 512], BF16, kind="Internal").ap()
    cDN = dt("cDN", [NL, 4, 4, 128, 16, 512], BF16, kind="Internal").ap()
    cGT = dt("cGT", [NL, 4, 128, 16, 512], BF16, kind="Internal").ap()

    P = Prog(nc)
    from contextlib import ExitStack

    with ExitStack() as es:
        def sb(name, shape, dtype):
            return es.enter_context(nc.sbuf_tensor(name, shape, dtype))

        X = sb("X", [128, NCH, TT], F32)
        H = sb("H", [128, NCH, TT], BF16)
        WS = [sb(f"WS{i}", [128, WSLOT], BF16) for i in range(NWS)]
        SMW = sb("SMW", [128, 6144], BF16)
        MIX = sb("MIX", [128, 24, TT], BF16)
        VF = sb("VF", [128, 8, TT], F32)
        SCR = sb("SCR", [128, 64 * TT], BF16)
        VEC = sb("VEC", [128, DEPTH, NV], F32)
        CB = sb("CB", [128, NCB], BF16)
        CF = sb("CF", [128, NCF], F32)
        SQ = sb("SQ", [128, 2, TT], BF16)
        RSTD = sb("RSTD", [128, TT], F32)
        CARRY = sb("CARRY", [128, DEPTH, 28], F32)
        CARRYP = sb("CARRYP", [128, DEPTH, 8, 16], BF16)
        ST = sb("ST", [64, DEPTH, 16, 64], BF16)
        LWA = sb("LWA", [128, TT], BF16)
        LG0 = sb("LG0", [128, TT], BF16)
        LG1 = sb("LG1", [32, TT], BF16)
        LVR = sb("LVR", [32, TT], BF16)
        ZP = sb("ZP", [128, 8, 16 + TT], BF16)
        QC = sb("QC", [128, 1, 512], BF16)
        QH = sb("QH", [16, 1, 512], BF16)
        TMALL = sb("TMALL", [64, 4, NSUB, 4, 128], BF16)
        DGE = sb("DGE", [128, 4, NSUB, 64], BF16)
        RKW = sb("RKW", [128, 8, 64], BF16)
        CH = sb("CH", [64, 14, 8, 64], BF16)
        Y32 = sb("Y32", [64, 8, TT], F32)
        OT = sb("OT", [64, 6, TT], F32)
        PTB = sb("PTB", [128, 2, TT], BF16)
        PJW = sb("PJW", [128, 2, 2048], BF16)
        TBUF = sb("TBUF", [128, 2, TT + 1], F32)
        PS = [es.enter_context(nc.psum_tensor(f"PS{i}", [128, 512], F32)) for i in range(8)]

        bPS = [Buf(f"ps{i}") for i in range(8)]
        bX = [Buf(f"X{c}") for c in range(NCH)]
        bH = [Buf(f"H{c}") for c in range(NCH)]
        bWS = [Buf(f"ws{i}") for i in range(NWS)]
        bSMW = Buf("smw")
        bMIX = [Buf(f"mix{c}") for c in range(24)]
        bVF = [Buf(f"vf{c}") for c in range(8)]
        bVEC, bCB, bCF = Buf("vec"), Buf("cb"), Buf("cf")
        bSQ = [Buf("sq0"), Buf("sq1")]
        bRSTD = Buf("rstd")
        bCARRY = [[Buf(f"cy{l}_{i}") for i in range(28)] for l in range(DEPTH)]
        bCARRYP = [Buf(f"cyp{l}") for l in range(DEPTH)]
        bST = [[Buf(f"st{l}_{h}") for h in range(16)] for l in range(DEPTH)]
        bLWA, bLG0, bLG1, bLVR = Buf("lwa"), Buf("lg0"), Buf("lg1"), Buf("lvr")
        bZP = [Buf(f"zp{c}") for c in range(8)]
        bQC = [Buf("qc0"), Buf("qc1")]
        bQH = [Buf("qh0"), Buf("qh1")]
        bTM = [[Buf(f"tm{m}_{s}") for s in range(NSUB)] for m in range(4)]
        bDGE = [Buf(f"dge{m}") for m in range(4)]
        bRKW = Buf("rkw")
        bCH = [Buf(f"ch{i}") for i in range(14)]
        bY32 = [Buf(f"y{h}") for h in range(8)]
        bOT = [Buf(f"ot{i}") for i in range(8)]
        bPTB = Buf("ptb")
        bPJW = Buf("pjw")
        bOST = [Buf("ost0"), Buf("ost1")]
        bTB = [Buf("tb0"), Buf("tb1")]
        bCV = [Buf(f"cv{l}") for l in range(DEPTH)]
        bPHASE = Buf("phase")

        HID = SCR[:, :].rearrange("p (c t) -> p c t", t=TT)
        bHID = [Buf(f"hid{c}") for c in range(64)]
        _scr_off = [0]

        def scr_bf(n):
            o = _scr_off[0]
            _scr_off[0] += n
            return SCR[:, o:o + n]

        def scr_f32(n):
            return scr_bf(2 * n).bitcast(F32)

        NTMP = 18
        TMP = [scr_f32(TT) for _ in range(NTMP)]
        bTMP = [Buf(f"tmp{i}") for i in range(NTMP)]
        TMPB = [scr_bf(TT) for _ in range(4)]
        bTMPB = [Buf(f"tmpb{i}") for i in range(4)]
        GOP = {}
        bGOP = {}
        for nm in ("AT", "BT", "KT", "RT", "RK", "VB"):
            GOP[nm] = scr_bf(4 * TT).rearrange("p (m t) -> p m t", t=TT)
            bGOP[nm] = [Buf(f"{nm}{m}") for m in range(4)]
        assert _scr_off[0] <= 64 * TT, _scr_off[0]

        def ACT(out, in_, func, reads, writes, bias=None, scale=None):
            kw = {}
            if bias is not None:
                kw["bias"] = bias
            if scale is not None:
                kw["scale"] = scale
            P.op("act", lambda e: e.activation(out=out, in_=in_, func=func, **kw), reads, writes)

        def TTT(out, in0, in1, op, reads, writes, eng="dve"):
            P.op(eng, lambda e: e.tensor_tensor(out=out, in0=in0, in1=in1, op=op), reads, writes)

        def STT(out, in0, scalar, in1, op0, op1, reads, writes):
            P.op("dve", lambda e: e.scalar_tensor_tensor(out=out, in0=in0, scalar=scalar, in1=in1, op0=op0, op1=op1), reads, writes)

        def TS(out, in0, s1, s2, op0, op1, reads, writes):
            if op1 is None:
                P.op("dve", lambda e: e.tensor_scalar(out=out, in0=in0, scalar1=s1, scalar2=None, op0=op0), reads, writes)
            else:
                P.op("dve", lambda e: e.tensor_scalar(out=out, in0=in0, scalar1=s1, scalar2=s2, op0=op0, op1=op1), reads, writes)

        def CP(out, in_, reads, writes, eng="dve"):
            P.op(eng, lambda e: e.tensor_copy(out=out, in_=in_), reads, writes)

        def MM(out, lhsT, rhs, start, stop, reads, writes):
            P.op("pe", lambda e: e.matmul(out, lhsT=lhsT, rhs=rhs, start=start, stop=stop), reads, writes)

        def TR(out, in_, ident, reads, writes):
            P.op("pe", lambda e: e.transpose(out, in_, ident), reads, writes)

        def RECIP(out, in_, reads, writes):
            P.op("dve", lambda e: e.reciprocal(out=out, in_=in_), reads, writes)

        def MEMSET(ap, val, writes, eng="dve"):
            P.op(eng, lambda e: e.memset(ap, val), (), writes)

        def DMA(q, key, pairs, reads, writes):
            fns = [(lambda e, o=o, i=i: e.dma_start(out=o, in_=i)) for (o, i) in pairs]
            P.dma(q, key, fns, reads, writes)

        def vcol(l, c, rows=128):
            return VEC[0:rows, l, c:c + 1]

        ws_ptr = [0]

        def wload(src, n_free, view_fn, l):
            i = ws_ptr[0] % NWS
            ws_ptr[0] += 1
            dst = view_fn(WS[i][:, 0:n_free])
            DMA("sp", f"w{i}", [(dst, src)], [bCV[l]], [bWS[i]])
            return i, dst

        def convert_layer(l):
            pairs = [(cLB[l], wLB[l])]
            pairs += [(cPB[l, i], wPB[l, i]) for i in range(2)]
            pairs += [(cRKV[l, i], wRKV[l, i]) for i in range(8)]
            pairs += [(cWO[l, i], wWO[l, i]) for i in range(8)]
            pairs += [(cUP[l, i], wUP[l, i]) for i in range(16)]
            pairs += [(cDN[l, i, j], wDN[l, i, j]) for i in range(4) for j in range(4)]
            pairs += [(cGT[l, i], wGT[l, i]) for i in range(4)]
            DMA("pool", f"cv{l}", pairs, [], [bCV[l]])

        DMA("sp", "vec", [(VEC[:, :, :], vecd[:, :, :])], [], [bVEC])
        DMA("sp", "cf", [(CF[:, :], cfd[:, :])], [], [bCF])
        DMA("pool", "cb", [(CB[:, :], cbd[:, :])], [], [bCB])
        for l in range(NL):
            TS(VEC[:, l, V_OMM:V_OMM + 28], VEC[:, l, V_MU:V_MU + 28], -1.0, 1.0, ALU.mult, ALU.add, [bVEC], [bVEC])
            TS(VEC[:, l, V_OKA:V_OKA + 8], VEC[:, l, V_KA:V_KA + 8], -1.0, 1.0, ALU.mult, ALU.add, [bVEC], [bVEC])
        MEMSET(CARRY[:, :, :], 0.0, [b for row in bCARRY for b in row])
        MEMSET(CARRYP[:, :, :, :], 0.0, bCARRYP)
        MEMSET(ST[:, :, :, :], 0.0, [b for row in bST for b in row])
        MEMSET(MIX[:, :, :], 0.0, bMIX)

        convert_layer(0)
        ONESM = CB[:, CB_ONESM:CB_ONESM + 128]
        IDENTB = CB[:, CB_ID:CB_ID + 128]
        BONES = CB[:, CB_BONES:CB_BONES + 128]
        IDB = CB[:, CB_IDB:CB_IDB + 64]

        def rmsnorm(l, gcol0):
            for c in range(NCH):
                ACT(SQ[:, c % 2, :], X[:, c, :], AF.Square, [bX[c]], [bSQ[c % 2]])
                MM(PS[7][:, 0:TT], ONESM, SQ[:, c % 2, :], c == 0, c == NCH - 1, [bSQ[c % 2], bCB], [bPS[7]])
            ACT(RSTD[:, :], PS[7][:, 0:TT], AF.Sqrt, [bPS[7], bCF], [bRSTD], bias=CF[:, CF_EPSN:CF_EPSN + 1])
            RECIP(RSTD[:, :], RSTD[:, :], [bRSTD], [bRSTD])

        def norm_to_H(l, gcol0):
            rmsnorm(l, gcol0)
            for c in range(NCH):
                STT(H[:, c, :], X[:, c, :], vcol(l, gcol0 + c), RSTD[:, :], ALU.mult, ALU.mult,
                    [bX[c], bRSTD, bVEC], [bH[c]])

        tb_ptr = [0]

        def shift_evac(l, ps_ap, pbuf, rows, cidx, out_ap, out_bufs, extra_reads=()):
            k = tb_ptr[0] % 2
            tb_ptr[0] += 1
            TA = TMP[16 + k]
            bTA = bTMP[16 + k]
            TB = TBUF[:, k, :]
            CP(TB[0:rows, 0:1], CARRY[0:rows, l, cidx:cidx + 1], [bCARRY[l][cidx], bPHASE], [bTB[k]])
            ACT(TA[0:rows, :], ps_ap, AF.Identity, [pbuf, bVEC, bPHASE], [bTA], scale=vcol(l, V_OMM + cidx, rows))
            ACT(TB[0:rows, 1:TT + 1], ps_ap, AF.Identity, [pbuf, bVEC], [bTB[k]], scale=vcol(l, V_MU + cidx, rows))
            TTT(out_ap, TA[0:rows, :], TB[0:rows, 0:TT], ALU.add, [bTA, bTB[k], bPHASE] + list(extra_reads), out_bufs)
            CP(CARRY[0:rows, l, cidx:cidx + 1], TB[0:rows, TT:TT + 1], [bTB[k]], [bCARRY[l][cidx]])

        def big_mm(ps_ap, pbuf, wview_fn, wbuf, rhs_fn, rhs_bufs, nk):
            for k in range(nk):
                MM(ps_ap, wview_fn(k), rhs_fn(k), k == 0, k == nk - 1, [wbuf, rhs_bufs[k]], [pbuf])

        for tile in range(NT):
            t0 = tile * TT
            DMA("sp", "x", [(X[:, :, :], xT.rearrange("c p t -> p c t")[:, :, t0:t0 + TT])], [], bX)
            for l in range(NL):
                first_tile = tile == 0
                DMA("pool", "ptb", [(PTB[:, :, :], pT[l].rearrange("k p t -> p k t")[:, :, t0:t0 + TT])], [], [bPTB])
                norm_to_H(l, V_ATTN)
                if stop_after < 1:
                    continue
                si, wv = wload(cLB[l], 16 * 320, lambda a: a.rearrange("p (k c) -> p k c", c=320), l)
                P.barrier()
                specs = [(0, 128, 24, 0), (128, 128, 25, 1), (256, 32, 26, 2)]
                if l > 0:
                    specs.append((288, 32, 27, 3))
                for (c0, rows, cidx, bank) in specs:
                    big_mm(PS[bank][0:rows, 0:TT], bPS[bank], lambda k, c0=c0, rows=rows: wv[:, k, c0:c0 + rows], bWS[si],
                           lambda k: H[:, k, :], bH, NCH)
                for (c0, rows, cidx, bank) in specs:
                    zs = TMP[15]
                    shift_evac(l, PS[bank][0:rows, 0:TT], bPS[bank], rows, cidx, zs[0:rows, :], [bTMP[15]])
                    if cidx == 24:
                        ACT(LWA[0:64, :], zs[0:64, :], AF.Tanh, [bTMP[15]], [bLWA])
                        ACT(LWA[64:128, :], zs[64:128, :], AF.Identity, [bTMP[15]], [bLWA])
                    elif cidx == 25:
                        ACT(LG0[:, :], zs[:, :], AF.Sigmoid, [bTMP[15]], [bLG0])
                    elif cidx == 26:
                        ACT(LG1[:, :], zs[0:32, :], AF.Sigmoid, [bTMP[15]], [bLG1])
                    else:
                        ACT(LVR[:, :], zs[0:32, :], AF.Identity, [bTMP[15]], [bLVR])
                if stop_after < 2:
                    continue
                DMA("pool", "smw", [(SMW[:, :], wSM[l])], [], [bSMW])
                DMA("pool", "pjw", [(PJW[:, :, :], wPJ[l])], [], [bPJW])
                if first_tile and l + 1 < NL:
                    convert_layer(l + 1)
                WUP = SMW[0:64, 0:1024]
                AUP = SMW[64:128, 0:1024]
                GUP0 = SMW[:, 1024:2048]
                GUP1 = SMW[0:32, 2048:3072]
                VUP = SMW[0:32, 3072:4096]
                POOLW = SMW[:, 4096:6144].rearrange("p (g k d) -> p g k d", g=4, k=2)
                CP(ZP[:, :, 0:16], CARRYP[:, l, :, :], [bCARRYP[l]], bZP)
                for pb in range(2):
                    si, wv = wload(cPB[l, pb], 16 * 512, lambda a: a.rearrange("p (k c) -> p k c", c=512), l)
                    for j in range(4):
                        c = pb * 4 + j
                        bank = 4 + j
                        big_mm(PS[bank][:, 0:TT], bPS[bank], lambda k, j=j: wv[:, k, j * 128:(j + 1) * 128], bWS[si],
                               lambda k: H[:, k, :], bH, NCH)
                        ACT(ZP[:, c, 16:16 + TT], PS[bank][:, 0:TT], AF.Copy, [bPS[bank]], [bZP[c]])
                CP(CARRYP[:, l, :, :], ZP[:, :, TT:TT + 16], bZP, [bCARRYP[l]])
                if stop_after < 3:
                    continue
                for gp in range(2):
                    for tb in range(NTB):
                        for gg in range(2):
                            g = 2 * gp + gg
                            for kk in range(2):
                                MM(PS[4][:, gg * 256:(gg + 1) * 256], ZP[:, 2 * g + kk, 16 + tb * 128:16 + (tb + 1) * 128],
                                   POOLW[:, g, kk, :], kk == 0, kk == 1, [bZP[2 * g + kk], bSMW], [bPS[4]])
                        for gg in range(2):
                            g = 2 * gp + gg
                            for kk in range(2):
                                MM(PS[5][0:16, gg * 256:(gg + 1) * 256], ZP[:, 2 * g + kk, tb * 128:tb * 128 + 16],
                                   POOLW[:, g, kk, :], kk == 0, kk == 1, [bZP[2 * g + kk], bSMW], [bPS[5]])
                        qi = 0
                        ACT(QC[:, qi, :], PS[4][:, :], AF.Copy, [bPS[4]], [bQC[qi]])
                        CP(QH[:, qi, :], PS[5][0:16, :], [bPS[5]], [bQH[qi]])
                        for dmi in range(4):
                            g = 2 * gp + dmi // 2
                            pmc = CB_PMF if (first_tile and tb == 0) else CB_PMC
                            MM(PS[dmi][:, tb * 128:(tb + 1) * 128], QH[0:16, qi, dmi * 128:(dmi + 1) * 128],
                               CB[0:16, CB_PMH + g * 128:CB_PMH + (g + 1) * 128], True, False, [bQH[qi], bCB], [bPS[dmi]])
                            MM(PS[dmi][:, tb * 128:(tb + 1) * 128], QC[:, qi, dmi * 128:(dmi + 1) * 128],
                               CB[:, pmc + g * 128:pmc + (g + 1) * 128], False, True, [bQC[qi], bCB], [bPS[dmi]])
                    for dmi in range(4):
                        dm = gp * 4 + dmi
                        ACT(MIX[:, dm, :], PS[dmi][:, 0:TT], AF.Identity, [bPS[dmi], bVEC], [bMIX[dm]], scale=vcol(l, V_PSC + dm))
                if stop_after < 4:
                    continue
                CP(RKW[:, :, :], VEC[:, l, V_RK:V_RK + 8].unsqueeze(2).to_broadcast([128, 8, 64]), [bVEC], [bRKW])
                for grp in range(2):
                    for mloc in range(4):
                        m = grp * 4 + mloc
                        rwkv_prep = None
                        MM(PS[4][:, 0:TT], WUP[:, m * 128:(m + 1) * 128], LWA[0:64, :], True, True, [bSMW, bLWA], [bPS[4]])
                        MM(PS[5][:, 0:TT], AUP[:, m * 128:(m + 1) * 128], LWA[64:128, :], True, True, [bSMW, bLWA], [bPS[5]])
                        LD, A32, SG, CS, CSM, IGAM, GAM1, GAM, DREV, GREV = TMP[0:10]
                        bLD, bA32, bSG, bCS, bCSM, bIGAM, bGAM1, bGAM, bDREV, bGREV = bTMP[0:10]
                        ACT(LD, PS[4][:, 0:TT], AF.Sigmoid, [bPS[4], bVEC, bPHASE], [bLD], bias=vcol(l, V_W0 + m))
                        ACT(A32, PS[5][:, 0:TT], AF.Sigmoid, [bPS[5], bVEC, bPHASE], [bA32], bias=vcol(l, V_A0 + m))
                        if l > 0:
                            MM(PS[6][:, 0:TT], VUP[:, m * 128:(m + 1) * 128], LVR[:, :], True, True, [bSMW, bLVR], [bPS[6]])
                            ACT(SG, PS[6][:, 0:TT], AF.Sigmoid, [bPS[6], bVEC, bPHASE], [bSG], bias=vcol(l, V_V0 + m))
                        P.op("dve", lambda e, CS=CS, LD=LD: e.tensor_tensor_scan(out=CS, data0=CF[:, CF_RESET:CF_RESET + TT], data1=LD,
                                                                                 initial=0.0, op0=ALU.mult, op1=ALU.add),
                             [bCF, bLD, bPHASE], [bCS])
                        ACT(IGAM, CS, AF.Exp, [bCS, bPHASE], [bIGAM], scale=C0)
                        ACT(GAM, CS, AF.Exp, [bCS, bPHASE], [bGAM], scale=-C0)
                        TTT(CSM, CS, LD, ALU.subtract, [bCS, bLD, bPHASE], [bCSM])
                        ACT(GAM1, CSM, AF.Exp, [bCSM, bPHASE], [bGAM1], scale=-C0)
                        CS3 = CS.rearrange("p (s t) -> p s t", t=64)
                        TTT(DREV.rearrange("p (s t) -> p s t", t=64), CS3[:, :, 63:64].to_broadcast([128, NSUB, 64]), CS3,
                            ALU.subtract, [bCS, bPHASE], [bDREV])
                        ACT(GREV, DREV, AF.Exp, [bDREV, bPHASE], [bGREV], scale=-C0)
                        GAM3 = GAM.rearrange("p (s t) -> p s t", t=64)
                        TTT(DGE[:, mloc, :, :], CF[:, CF_ID2:CF_ID2 + 64].unsqueeze(1).to_broadcast([128, NSUB, 64]),
                            GAM3[:, :, 63:64].to_broadcast([128, NSUB, 64]), ALU.mult, [bCF, bGAM], [bDGE[mloc]])
                        si, wv = wload(cRKV[l, m], 16 * 384, lambda a: a.rearrange("p (k c) -> p k c", c=384), l)
                        for j in range(3):
                            big_mm(PS[4 + j][:, 0:TT], bPS[4 + j], lambda k, j=j: wv[:, k, j * 128:(j + 1) * 128], bWS[si],
                                   lambda k: H[:, k, :], bH, NCH)
                        R32, K32, V32, RN, KKN, B32 = TMP[10:16][0], TMP[11], TMP[12], TMP[13], TMP[14], TMP[15]
                        bR32, bK32, bV32, bRN, bKKN, bB32 = bTMP[10], bTMP[11], bTMP[12], bTMP[13], bTMP[14], bTMP[15]
                        shift_evac(l, PS[4][:, 0:TT], bPS[4], 128, 0 + m, R32, [bR32])
                        shift_evac(l, PS[5][:, 0:TT], bPS[5], 128, 8 + m, K32, [bK32])
                        shift_evac(l, PS[6][:, 0:TT], bPS[6], 128, 16 + m, V32, [bV32])
                        TTT(GOP["RT"][:, mloc, :], R32, GAM, ALU.mult, [bR32, bGAM, bPHASE], [bGOP["RT"][mloc]])
                        KSQ = TMPB[0]
                        ACT(KSQ, K32, AF.Square, [bK32, bVEC, bPHASE], [bTMPB[0]], scale=vcol(l, V_KK + m))
                        MM(PS[7][:, 0:TT], BONES, KSQ, True, True, [bCB, bTMPB[0]], [bPS[7]])
                        ACT(RN, PS[7][:, 0:TT], AF.Sqrt, [bPS[7], bPHASE], [bRN])
                        TS(RN, RN, 1e-12, None, ALU.max, None, [bRN], [bRN])
                        RECIP(RN, RN, [bRN], [bRN])
                        STT(KKN, K32, vcol(l, V_KK + m), RN, ALU.mult, ALU.mult, [bK32, bRN, bVEC, bPHASE], [bKKN])
                        STT(GOP["AT"][:, mloc, :], KKN, -1.0, GAM1, ALU.mult, ALU.mult, [bKKN, bGAM1, bPHASE], [bGOP["AT"][mloc]])
                        TTT(B32, KKN, A32, ALU.mult, [bKKN, bA32, bPHASE], [bB32])
                        TTT(GOP["BT"][:, mloc, :], B32, IGAM, ALU.mult, [bB32, bIGAM, bPHASE], [bGOP["BT"][mloc]])
                        BH = TMPB[1]
                        TTT(BH, B32, GREV, ALU.mult, [bB32, bGREV, bPHASE], [bTMPB[1]])
                        T1 = RN
                        TS(T1, A32, vcol(l, V_KA + m), vcol(l, V_OKA + m), ALU.mult, ALU.add, [bA32, bVEC, bPHASE], [bRN])
                        KP = KKN
                        TTT(KP, K32, T1, ALU.mult, [bK32, bRN, bPHASE], [bKKN])
                        TTT(GOP["KT"][:, mloc, :], KP, IGAM, ALU.mult, [bKKN, bIGAM, bPHASE], [bGOP["KT"][mloc]])
                        KH = TMPB[2]
                        TTT(KH, KP, GREV, ALU.mult, [bKKN, bGREV, bPHASE], [bTMPB[2]])
                        TTT(GOP["RK"][:, mloc, :], R32, KP, ALU.mult, [bR32, bKKN, bPHASE], [bGOP["RK"][mloc]])
                        if l == 0:
                            CP(VF[:, m, :], V32, [bV32, bPHASE], [bVF[m]])
                            CP(GOP["VB"][:, mloc, :], V32, [bV32, bPHASE], [bGOP["VB"][mloc]])
                        else:
                            TTT(B32, VF[:, m, :], V32, ALU.subtract, [bVF[m], bV32, bPHASE], [bB32])
                            TTT(B32, B32, SG, ALU.mult, [bB32, bSG, bPHASE], [bB32])
                            TTT(GOP["VB"][:, mloc, :], V32, B32, ALU.add, [bV32, bB32, bPHASE], [bGOP["VB"][mloc]])
                        srcs = [(GOP["AT"][:, mloc, :], bGOP["AT"][mloc]), (BH, bTMPB[1]), (KH, bTMPB[2]),
                                (GOP["VB"][:, mloc, :], bGOP["VB"][mloc])]
                        for sp in range(NSUB // 2):
                            bank = sp % 2
                            PSB = PS[bank][:, :].bitcast(BF16)
                            for s2 in range(2):
                                s = sp * 2 + s2
                                for qi, (src, sbf) in enumerate(srcs):
                                    TR(PSB[0:64, (s2 * 4 + qi) * 128:(s2 * 4 + qi + 1) * 128], src[:, s * 64:(s + 1) * 64], IDENTB,
                                       [sbf, bCB, bPHASE], [bPS[bank]])
                            CP(TMALL[:, mloc, sp * 2:sp * 2 + 2, :, :].rearrange("p s q c -> p (s q c)"), PSB[0:64, :],
                               [bPS[bank]], [bTM[mloc][sp * 2], bTM[mloc][sp * 2 + 1]])
                    if stop_after < 5:
                        continue
                    MU_S = CF[0:64, CF_MU_S:CF_MU_S + 64].unsqueeze(1).to_broadcast([64, 8, 64])
                    MU_I = CF[0:64, CF_MU_I:CF_MU_I + 64].unsqueeze(1).to_broadcast([64, 8, 64])
                    ML_S = CF[0:64, CF_ML_S:CF_ML_S + 64].unsqueeze(1).to_broadcast([64, 8, 64])
                    ID64 = CF[0:64, CF_ID64:CF_ID64 + 64].unsqueeze(1).to_broadcast([64, 8, 64])
                    bk = [0]
                    bko = [0]

                    def nbank():
                        b = bk[0] % 6
                        bk[0] += 1
                        return b

                    def nbank_o():
                        b = 6 + bko[0] % 2
                        bko[0] += 1
                        return b

                    def pview(b):
                        return PS[b][0:64, :].rearrange("p (h t) -> p h t", t=64)

                    def SELJ(j):
                        return CB[:, CB_SEL + 64 * j:CB_SEL + 64 * (j + 1)]

                    for s in range(NSUB):
                        tc0 = s * 64

                        def fm(nm, hi):
                            j, ml = hi // 4, hi % 4
                            return GOP[nm][64 * j:64 * j + 64, ml, tc0:tc0 + 64], bGOP[nm][ml]

                        def tm(q, hi):
                            j, ml = hi // 4, hi % 4
                            return TMALL[:, ml, s, q, 64 * j:64 * j + 64], bTM[ml][s]

                        (cM, cN, cARB, cAAK, cARK, cP0, cP1, cM2, cN2, cX, cQQ, cPP, cRP, cGG) = range(14)

                        def amat(lname, rname, ci, mask):
                            be = nbank()
                            bo = nbank_o()
                            for hi in range(8):
                                la, lb = fm(lname, hi)
                                ra, rb = fm(rname, hi)
                                b = be if hi < 4 else bo
                                MM(pview(b)[:, hi % 4, :], la, ra, True, True, [lb, rb, bPHASE], [bPS[b]])
                            TTT(CH[:, ci, 0:4, :], pview(be)[:, 0:4, :], mask[:, 0:4, :], ALU.mult, [bPS[be], bCF], [bCH[ci]])
                            TTT(CH[:, ci, 4:8, :], pview(bo)[:, 0:4, :], mask[:, 0:4, :], ALU.mult, [bPS[bo], bCF], [bCH[ci]])

                        amat("BT", "AT", cM, MU_S)
                        amat("AT", "BT", cN, ML_S)
                        amat("BT", "RT", cARB, MU_I)
                        amat("KT", "AT", cAAK, MU_S)
                        amat("KT", "RT", cARK, MU_I)
                        TTT(CH[:, cP0, :, :], CH[:, cM, :, :], ID64, ALU.add, [bCH[cM], bCF], [bCH[cP0]])
                        curM, curN, curP = cM, cN, cP0
                        altM, altN, altP = cM2, cN2, cP1
                        for lev in range(5):
                            bN = nbank()
                            for hi in range(8):
                                MM(pview(bN)[:, hi, :], CH[:, curM, hi, :], CH[:, curN, hi, :], True, True,
                                   [bCH[curM], bCH[curN]], [bPS[bN]])
                            ACT(CH[:, altN, :, :], pview(bN), AF.Copy, [bPS[bN]], [bCH[altN]])
                            if lev < 4:
                                bM = nbank()
                                for hi in range(8):
                                    MM(pview(bM)[:, hi, :], CH[:, curN, hi, :], CH[:, curM, hi, :], True, True,
                                       [bCH[curM], bCH[curN]], [bPS[bM]])
                                ACT(CH[:, altM, :, :], pview(bM), AF.Copy, [bPS[bM]], [bCH[altM]])
                            bP = nbank()
                            for hi in range(8):
                                MM(pview(bP)[:, hi, :], CH[:, altN, hi, :], CH[:, curP, hi, :], True, True,
                                   [bCH[altN], bCH[curP]], [bPS[bP]])
                            TTT(CH[:, altP, :, :], CH[:, curP, :, :], pview(bP), ALU.add, [bCH[curP], bPS[bP]], [bCH[altP]])
                            curM, altM = altM, curM
                            curN, altN = altN, curN
                            curP, altP = altP, curP
                        cTt = curP
                        b = nbank()
                        for hi in range(8):
                            va, vb = tm(3, hi)
                            MM(pview(b)[:, hi, :], CH[:, cAAK, hi, :], va, True, True, [bCH[cAAK], vb], [bPS[b]])
                        ACT(CH[:, cX, :, :], pview(b), AF.Copy, [bPS[b]], [bCH[cX]])
                        b = nbank()
                        for hi in range(8):
                            MM(pview(b)[:, hi, :], CH[:, cTt, hi, :], CH[:, cX, hi, :], True, True, [bCH[cTt], bCH[cX]], [bPS[b]])
                        ACT(CH[:, cQQ, :, :], pview(b), AF.Copy, [bPS[b]], [bCH[cQQ]])
                        b = nbank()
                        for hi in range(8):
                            aa, ab = tm(0, hi)
                            MM(pview(b)[:, hi, :], CH[:, cTt, hi, :], aa, True, True, [bCH[cTt], ab], [bPS[b]])
                        CP(CH[:, cPP, :, :], pview(b), [bPS[b]], [bCH[cPP]])
                        b = nbank()
                        for hi in range(8):
                            j, ml = hi // 4, hi % 4
                            MM(pview(b)[:, hi, :], SELJ(j), GOP["RT"][:, ml, tc0:tc0 + 64], True, False, [bCB, bGOP["RT"][ml]], [bPS[b]])
                            MM(pview(b)[:, hi, :], CH[:, cPP, hi, :], CH[:, cARB, hi, :], False, True, [bCH[cPP], bCH[cARB]], [bPS[b]])
                        ACT(CH[:, cRP, :, :], pview(b), AF.Copy, [bPS[b]], [bCH[cRP]])
                        b = nbank()
                        for hi in range(8):
                            j, ml = hi // 4, hi % 4
                            ba, bb = tm(1, hi)
                            MM(pview(b)[:, hi, :], SELJ(j), DGE[:, ml, s, :], True, False, [bCB, bDGE[ml]], [bPS[b]])
                            MM(pview(b)[:, hi, :], CH[:, cPP, hi, :], ba, False, True, [bCH[cPP], bb], [bPS[b]])
                        CP(CH[:, cGG, :, :], pview(b), [bPS[b]], [bCH[cGG]])
                        b = nbank()
                        stb = [bST[l][grp * 8 + hi] for hi in range(8)]
                        for hi in range(8):
                            h = grp * 8 + hi
                            va, vb = tm(3, hi)
                            MM(pview(b)[:, hi, :], CH[:, cQQ, hi, :], CH[:, cARB, hi, :], True, False, [bCH[cQQ], bCH[cARB]], [bPS[b]])
                            MM(pview(b)[:, hi, :], va, CH[:, cARK, hi, :], False, False, [vb, bCH[cARK]], [bPS[b]])
                            MM(pview(b)[:, hi, :], ST[:, l, h, :], CH[:, cRP, hi, :], False, True, [stb[hi], bCH[cRP]], [bPS[b]])
                        ACT(Y32[:, :, tc0:tc0 + 64], pview(b), AF.Copy, [bPS[b]], bY32)
                        b = nbank()
                        for hi in range(8):
                            h = grp * 8 + hi
                            ba, bb = tm(1, hi)
                            ka, kb = tm(2, hi)
                            va, vb = tm(3, hi)
                            MM(pview(b)[:, hi, :], ba, CH[:, cQQ, hi, :], True, False, [bb, bCH[cQQ]], [bPS[b]])
                            MM(pview(b)[:, hi, :], ka, va, False, False, [kb, vb], [bPS[b]])
                            MM(pview(b)[:, hi, :], CH[:, cGG, hi, :], ST[:, l, h, :], False, True, [bCH[cGG], stb[hi]], [bPS[b]])
                        CP(ST[:, l, grp * 8:grp * 8 + 8, :], pview(b), [bPS[b]], stb)
                    if stop_after < 6:
                        continue
                    for hh in range(8):
                        j, ml = hh // 4, hh % 4
                        h = grp * 8 + 2 * ml + j
                        bc_, bv_ = (2, 3) if j == 0 else (6, 7)
                        yh = Y32[:, hh, :]
                        YSQ, MUt, T1o, RS, Dd, CBt = [OT[:, i, :] for i in range(6)]
                        ACT(YSQ, yh, AF.Square, [bY32[hh]], [bOT[0]])
                        O64 = CF[0:64, CF_ONES64:CF_ONES64 + 64]
                        MM(PS[0][0:64, 0:TT], O64, yh, True, True, [bCF, bY32[hh]], [bPS[0]])
                        MM(PS[1][0:64, 0:TT], O64, YSQ, True, True, [bCF, bOT[0]], [bPS[1]])
                        ACT(MUt, PS[0][0:64, 0:TT], AF.Copy, [bPS[0]], [bOT[1]])
                        TTT(T1o, MUt, MUt, ALU.mult, [bOT[1]], [bOT[2]])
                        TTT(T1o, PS[1][0:64, 0:TT], T1o, ALU.subtract, [bPS[1], bOT[2]], [bOT[2]])
                        ACT(RS, T1o, AF.Sqrt, [bOT[2], bCF], [bOT[3]], bias=CF[0:64, CF_EPSG:CF_EPSG + 1])
                        RECIP(RS, RS, [bOT[3]], [bOT[3]])
                        TTT(Dd, yh, MUt, ALU.subtract, [bY32[hh], bOT[1]], [bOT[4]])
                        TTT(Dd, Dd, RS, ALU.mult, [bOT[4], bOT[3]], [bOT[4]])
                        TS(Dd, Dd, VEC[0:64, l, V_GNG + h:V_GNG + h + 1], VEC[0:64, l, V_GNB + h:V_GNB + h + 1], ALU.mult, ALU.add,
                           [bOT[4], bVEC], [bOT[4]])
                        MM(PS[bc_][0:64, 0:TT], RKW[64 * j:64 * j + 64, grp * 4 + ml, :], GOP["RK"][64 * j:64 * j + 64, ml, :], True, True,
                           [bRKW, bGOP["RK"][ml], bPHASE], [bPS[bc_]])
                        MM(PS[bv_][0:64, 0:TT], SELJ(j), GOP["VB"][:, ml, :], True, True,
                           [bCB, bGOP["VB"][ml], bPHASE], [bPS[bv_]])
                        ACT(CBt, PS[bc_][0:64, 0:TT], AF.Copy, [bPS[bc_]], [bOT[5]])
                        TTT(CBt, CBt, PS[bv_][0:64, 0:TT], ALU.mult, [bOT[5], bPS[bv_]], [bOT[5]])
                        TTT(Dd, Dd, CBt, ALU.add, [bOT[4], bOT[5]], [bOT[4]])
                        gb = 4 + hh % 2
                        MM(PS[gb][0:64, 0:TT], GUP0[:, h * 64:(h + 1) * 64], LG0[:, :], True, False, [bSMW, bLG0], [bPS[gb]])
                        MM(PS[gb][0:64, 0:TT], GUP1[:, h * 64:(h + 1) * 64], LG1[:, :], False, True, [bSMW, bLG1], [bPS[gb]])
                        TTT(MIX[0:64, 8 + h, :], Dd, PS[gb][0:64, 0:TT], ALU.mult, [bOT[4], bPS[gb]], [bMIX[8 + h]])
                if stop_after < 7:
                    continue
                for cb in range(8):
                    si, wv = wload(cWO[l, cb], 24 * 256, lambda a: a.rearrange("p (k c) -> p k c", c=256), l)
                    for oc2 in range(2):
                        oc = cb * 2 + oc2
                        bank = oc % 4
                        for k in range(8):
                            MM(PS[bank][:, 0:TT], wv[:, k, oc2 * 128:(oc2 + 1) * 128], MIX[:, k, :], k == 0, False,
                               [bWS[si], bMIX[k]], [bPS[bank]])
                        for h in range(16):
                            MM(PS[bank][:, 0:TT], wv[0:64, 8 + h, oc2 * 128:(oc2 + 1) * 128], MIX[0:64, 8 + h, :], False, h == 15,
                               [bWS[si], bMIX[8 + h]], [bPS[bank]])
                        TTT(X[:, oc, :], X[:, oc, :], PS[bank][:, 0:TT], ALU.add, [bX[oc], bPS[bank]], [bX[oc]])
                if stop_after < 8:
                    continue
                norm_to_H(l, V_MLP)
                P.barrier()
                first_hid = False
                for ub in range(16):
                    si, wv = wload(cUP[l, ub], 16 * 512, lambda a: a.rearrange("p (k c) -> p k c", c=512), l)
                    for hc4 in range(4):
                        hc = ub * 4 + hc4
                        bank = 4 + hc % 4
                        big_mm(PS[bank][:, 0:TT], bPS[bank], lambda k, hc4=hc4: wv[:, k, hc4 * 128:(hc4 + 1) * 128], bWS[si],
                               lambda k: H[:, k, :], bH, NCH)
                        k2 = hc % 2
                        ACT(SQ[:, k2, :], PS[bank][:, 0:TT], AF.Relu, [bPS[bank]], [bSQ[k2]])
                        if first_hid:
                            TTT(HID[:, hc, :], SQ[:, k2, :], SQ[:, k2, :], ALU.mult, [bSQ[k2]], [bHID[hc], bPHASE])
                            first_hid = False
                        else:
                            TTT(HID[:, hc, :], SQ[:, k2, :], SQ[:, k2, :], ALU.mult, [bSQ[k2], bPHASE], [bHID[hc]])
                for cb in range(4):
                    for kg in range(4):
                        si, wv = wload(cDN[l, cb, kg], 16 * 512, lambda a: a.rearrange("p (k c) -> p k c", c=512), l)
                        for oc4 in range(4):
                            for k in range(16):
                                MM(PS[oc4][:, 0:TT], wv[:, k, oc4 * 128:(oc4 + 1) * 128], HID[:, kg * 16 + k, :],
                                   kg == 0 and k == 0, kg == 3 and k == 15, [bWS[si], bHID[kg * 16 + k], bPHASE], [bPS[oc4]])
                    for oc4 in range(4):
                        oc = cb * 4 + oc4
                        TTT(X[:, oc, :], X[:, oc, :], PS[oc4][:, 0:TT], ALU.add, [bX[oc], bPS[oc4]], [bX[oc]])
                if stop_after < 9:
                    continue
                norm_to_H(l, V_PLE)
                pj = PJW
                for cb in range(4):
                    si, wv = wload(cGT[l, cb], 16 * 512, lambda a: a.rearrange("p (k c) -> p k c", c=512), l)
                    for oc4 in range(4):
                        oc = cb * 4 + oc4
                        bank = 4 + oc4
                        big_mm(PS[bank][:, 0:TT], bPS[bank], lambda k, oc4=oc4: wv[:, k, oc4 * 128:(oc4 + 1) * 128], bWS[si],
                               lambda k: H[:, k, :], bH, NCH)
                        for k in range(2):
                            MM(PS[oc4][:, 0:TT], pj[:, k, oc * 128:(oc + 1) * 128], PTB[:, k, :], k == 0, k == 1,
                               [bPJW, bPTB], [bPS[oc4]])
                        gt = TBUF[:, oc4 % 2, 0:TT]
                        ACT(gt, PS[bank][:, 0:TT], AF.Sigmoid, [bPS[bank]], [bTB[oc4 % 2]])
                        TTT(gt, gt, PS[oc4][:, 0:TT], ALU.mult, [bTB[oc4 % 2], bPS[oc4]], [bTB[oc4 % 2]])
                        TTT(X[:, oc, :], X[:, oc, :], gt, ALU.add, [bX[oc], bTB[oc4 % 2]], [bX[oc]])
            rmsnorm(0, V_FIN)
            for c in range(NCH):
                k = c % 2
                STT(TBUF[:, k, 0:TT], X[:, c, :], vcol(0, V_FIN + c), RSTD[:, :], ALU.mult, ALU.mult, [bX[c], bRSTD, bVEC], [bTB[k]])
                DMA("sp", f"o{k}", [(oT[c, :, t0:t0 + TT], TBUF[:, k, 0:TT])], [bTB[k]], [])
        P.wait_all("sp", bTB)
        P.emit()
    return nc


def _blk(w, nk):
    return np.ascontiguousarray(w.reshape(nk, 128, w.shape[1]).transpose(1, 0, 2))


def make_consts():
    cb = np.zeros((128, NCB), np.float32)
    cb[:, CB_ID:CB_ID + 128] = np.eye(128)
    cb[:, CB_ONESM:CB_ONESM + 128] = 1.0 / 2048
    blk = np.arange(128) // 64
    cb[:, CB_BONES:CB_BONES + 128] = (blk[:, None] == blk[None, :]).astype(np.float32)
    cb[:, CB_IDB:CB_IDB + 64] = (np.arange(128)[:, None] % 64 == np.arange(64)[None, :]).astype(np.float32)
    for j in range(2):
        cb[:, CB_SEL + j * 64:CB_SEL + (j + 1) * 64] = (np.arange(128)[:, None] == 64 * j + np.arange(64)[None, :]).astype(np.float32)
    t = np.arange(128)
    for g, w in enumerate((2, 4, 8, 16)):
        dlt = t[None, :] - t[:, None]
        band = ((dlt >= 0) & (dlt < w)).astype(np.float32)
        cb[:, CB_PMC + g * 128:CB_PMC + (g + 1) * 128] = band / w - np.eye(128)
        cnt = np.minimum(t + 1, w).astype(np.float32)
        cb[:, CB_PMF + g * 128:CB_PMF + (g + 1) * 128] = band / cnt[None, :] - np.eye(128)
        i = np.arange(16)
        dh = t[None, :] + 16 - i[:, None]
        cb[0:16, CB_PMH + g * 128:CB_PMH + (g + 1) * 128] = ((dh < w)).astype(np.float32) / w
    cf = np.zeros((128, NCF), np.float32)
    r = np.arange(64)
    cf[0:64, CF_MU_S:CF_MU_S + 64] = (r[None, :] > r[:, None])
    cf[0:64, CF_MU_I:CF_MU_I + 64] = (r[None, :] >= r[:, None])
    cf[0:64, CF_ML_S:CF_ML_S + 64] = (r[None, :] < r[:, None])
    cf[0:64, CF_ID64:CF_ID64 + 64] = np.eye(64)
    cf[:, CF_RESET:CF_RESET + TT] = (np.arange(TT) % 64 != 0).astype(np.float32)[None, :]
    cf[:, CF_ID2:CF_ID2 + 64] = (np.arange(128)[:, None] % 64 == np.arange(64)[None, :])
    cf[0:64, CF_ONES64:CF_ONES64 + 64] = 1.0 / 64
    cf[:, CF_EPSN] = NORM_EPS
    cf[:, CF_EPSG] = GN_EPS
    return cb, cf


def prep_weights(inp, NL=DEPTH):
    L = NL
    f = np.float32
    w_in = inp["w_in"]
    out = {}
    wLB = np.zeros((L, 128, 16, 320), f)
    wPB = np.zeros((L, 2, 128, 16, 512), f)
    wRKV = np.zeros((L, 8, 128, 16, 384), f)
    wSM = np.zeros((L, 128, 6144), f)
    wWO = np.zeros((L, 8, 128, 24, 256), f)
    wUP = np.zeros((L, 16, 128, 16, 512), f)
    wDN = np.zeros((L, 4, 4, 128, 16, 512), f)
    wGT = np.zeros((L, 4, 128, 16, 512), f)
    wPJ = np.zeros((L, 128, 2, 2048), f)
    vec = np.zeros((128, DEPTH, NV), f)
    for l in range(L):
        W = w_in[l]
        lb = W[:, 4096:4384]
        if l > 0:
            lb = np.concatenate([lb, inp["w_vres_dn"][l - 1]], axis=1)
        else:
            lb = np.concatenate([lb, np.zeros((D, 32), f)], axis=1)
        wLB[l] = _blk(lb, 16)
        for pb in range(2):
            wPB[l, pb] = _blk(W[:, pb * 512:(pb + 1) * 512], 16)
        for m in range(8):
            cols = np.concatenate([W[:, 1024 + j * 1024 + m * 128:1024 + j * 1024 + (m + 1) * 128] for j in range(3)], axis=1)
            wRKV[l, m] = _blk(cols, 16)
        wSM[l, 0:64, 0:1024] = inp["w_up"][l]
        wSM[l, 64:128, 0:1024] = inp["a_up"][l]
        wSM[l, :, 1024:2048] = inp["g_up"][l][0:128]
        wSM[l, 0:32, 2048:3072] = inp["g_up"][l][128:160]
        if l > 0:
            wSM[l, 0:32, 3072:4096] = inp["v_up"][l - 1]
        pw = inp["pool_w"][l]
        wSM[l, :, 4096:6144] = pw.reshape(4, 2, 128, 256).transpose(2, 0, 1, 3).reshape(128, 2048)
        wo = inp["w_out"][l]
        for cb in range(8):
            sl = wo[:, cb * 256:(cb + 1) * 256]
            wWO[l, cb, :, 0:8, :] = sl[0:1024].reshape(8, 128, 256).transpose(1, 0, 2)
            wWO[l, cb, 0:64, 8:24, :] = sl[1024:2048].reshape(16, 64, 256).transpose(1, 0, 2)
        wu = inp["w_ffn_up"][l]
        for ub in range(16):
            wUP[l, ub] = _blk(wu[:, ub * 512:(ub + 1) * 512], 16)
        wd = inp["w_ffn_down"][l]
        for cb in range(4):
            for kg in range(4):
                wDN[l, cb, kg] = _blk(wd[kg * 2048:(kg + 1) * 2048, cb * 512:(cb + 1) * 512], 16)
        wg = inp["w_ple_gate"][l]
        for cb in range(4):
            wGT[l, cb] = _blk(wg[:, cb * 512:(cb + 1) * 512], 16)
        wPJ[l] = _blk(inp["w_ple_proj"][l], 2)
        vec[:, l, V_ATTN:V_ATTN + 16] = inp["attn_norm"][l].reshape(16, 128).T
        vec[:, l, V_MLP:V_MLP + 16] = inp["mlp_norm"][l].reshape(16, 128).T
        vec[:, l, V_PLE:V_PLE + 16] = inp["ple_norm"][l].reshape(16, 128).T
        vec[:, l, V_FIN:V_FIN + 16] = inp["final_norm"].reshape(16, 128).T
        mu = inp["mu_shift"][l]
        vec[:, l, V_MU:V_MU + 24] = mu[0:3072].reshape(24, 128).T
        vec[:, l, V_MU + 24] = mu[3072:3200]
        vec[:, l, V_MU + 25] = mu[3200:3328]
        vec[0:32, l, V_MU + 26] = mu[3328:3360]
        if l > 0:
            vec[0:32, l, V_MU + 27] = inp["mu_vres"][l - 1]
        vec[:, l, V_PSC:V_PSC + 8] = inp["pool_scale"][l].reshape(8, 128).T
        vec[:, l, V_W0:V_W0 + 8] = inp["w0"][l].reshape(8, 128).T
        vec[:, l, V_A0:V_A0 + 8] = inp["a0"][l].reshape(8, 128).T
        if l > 0:
            vec[:, l, V_V0:V_V0 + 8] = inp["v0"][l - 1].reshape(8, 128).T
        vec[:, l, V_KK:V_KK + 8] = inp["k_k"][l].reshape(8, 128).T
        vec[:, l, V_KA:V_KA + 8] = inp["k_a"][l].reshape(8, 128).T
        vec[:, l, V_RK:V_RK + 8] = inp["r_k"][l].reshape(8, 128).T
        vec[0:64, l, V_GNG:V_GNG + 16] = inp["gn_g"][l].reshape(16, 64).T
        vec[0:64, l, V_GNB:V_GNB + 16] = inp["gn_b"][l].reshape(16, 64).T
    return dict(wLB=wLB, wPB=wPB, wRKV=wRKV, wSM=wSM, wWO=wWO, wUP=wUP, wDN=wDN, wGT=wGT, wPJ=wPJ, vec=vec)


def run_device(inputs, T_run=SEQ, NL=DEPTH, n_cores=8, stop_after=99):
    inp = {k: np.asarray(v, dtype=np.float32) for k, v in inputs.items()}
    wd = prep_weights(inp, NL)
    cb, cf = make_consts()
    NT = T_run // TT
    nc = build_program(NT, NL, T_run, stop_after)
    in_maps = []
    for core in range(n_cores):
        b = (core // 2) % BATCH
        xT = np.ascontiguousarray(inp["x"][b, :T_run].T).reshape(NCH, 128, T_run)
        pT = np.ascontiguousarray(inp["p"][:, b, :T_run].transpose(0, 2, 1)).reshape(DEPTH, 2, 128, T_run)
        m = dict(xT=xT, pT=pT, cbf=cb, cf32=cf)
        m.update(wd)
        in_maps.append(m)
    res = run_bass_kernel_spmd(nc, in_maps, core_ids=list(range(n_cores)))
    outs = []
    for b in range(BATCH):
        o = res.results[2 * b]["oT"] if n_cores == 8 else res.results[min(2 * b, n_cores - 1)]["oT"]
        outs.append(np.asarray(o).reshape(D, T_run).T)
    return np.stack(outs, axis=0).astype(np.float32)


def kernel(**inputs):
    return run_device(inputs)
```

```python
import numpy as np
import concourse.bass as bass
import concourse.mybir as mybir
from concourse.bass_utils import run_bass_kernel_spmd

F32 = mybir.dt.float32
BF16 = mybir.dt.bfloat16
AF = mybir.ActivationFunctionType
ALU = mybir.AluOpType

D = 2048
SEQ = 4096
BATCH = 4
DEPTH = 4
TT = 256
NSUB = TT // 64
NTB = TT // 128
NCH = 16
C0 = float(np.exp(-0.5))
NORM_EPS = 1e-6
GN_EPS = 64e-5
WSLOT = 8192
NWS = 2

V_ATTN, V_MLP, V_PLE, V_FIN = 0, 16, 32, 48
V_MU = 64
V_OMM = 92
V_PSC = 120
V_W0, V_A0, V_V0, V_KK, V_KA, V_OKA, V_RK = 128, 136, 144, 152, 160, 168, 176
V_GNG, V_GNB = 184, 200
NV = 216

CB_ID = 0
CB_ONESM = 128
CB_BONES = 256
CB_IDB = 384
CB_PMC = 448
CB_PMH = 960
CB_PMF = 1472
CB_SEL = 1984
NCB = 2112
CF_MU_S = 0
CF_MU_I = 64
CF_ML_S = 128
CF_ID64 = 192
CF_RESET = 256
CF_ID2 = CF_RESET + TT
CF_ONES64 = CF_ID2 + 64
CF_EPSN = CF_ONES64 + 64
CF_EPSG = CF_EPSN + 1
NCF = CF_EPSG + 1


class Buf:
    __slots__ = ("w", "r", "name")

    def __init__(self, name=""):
        self.w = None
        self.r = {}
        self.name = name


class Prog:
    ENG = ["pe", "act", "dve", "pool", "sp"]

    def __init__(self, nc):
        self.nc = nc
        self.q = {e: [] for e in self.ENG}
        self.cnt = {e: 0 for e in self.ENG}
        self.sem = {e: nc.alloc_semaphore("s_" + e) for e in self.ENG}
        self.waited = {e: {} for e in self.ENG}
        self.dsem = {}
        self.dcnt = {}

    def _semobj(self, key):
        return self.sem[key] if key in self.sem else self.dsem[key]

    def _waits(self, eng, reads, writes):
        deps = {}
        for b in reads:
            if b.w is not None:
                k, v = b.w
                if deps.get(k, 0) < v:
                    deps[k] = v
        for b in writes:
            if b.w is not None:
                k, v = b.w
                if deps.get(k, 0) < v:
                    deps[k] = v
            for k, v in b.r.items():
                if deps.get(k, 0) < v:
                    deps[k] = v
        waits = []
        wd = self.waited[eng]
        for k, v in deps.items():
            if k == eng and eng == "pe":
                continue
            if wd.get(k, 0) < v:
                waits.append((k, v))
                wd[k] = v
        return waits

    def op(self, eng, fn, reads=(), writes=()):
        waits = self._waits(eng, reads, writes)
        self.cnt[eng] += 1
        c = self.cnt[eng]
        self.q[eng].append((waits, fn, None, c))
        for b in reads:
            if b.r.get(eng, 0) < c:
                b.r[eng] = c
        for b in writes:
            b.w = (eng, c)
            b.r = {}

    def dma(self, q, key, fns, reads=(), writes=()):
        if key not in self.dsem:
            self.dsem[key] = self.nc.alloc_semaphore("d_" + key)
            self.dcnt[key] = 0
        waits = self._waits(q, reads, writes)
        self.dcnt[key] += 16 * len(fns)
        v = self.dcnt[key]
        for i, fn in enumerate(fns):
            self.q[q].append((waits if i == 0 else [], fn, key, 0))
        for b in reads:
            if b.r.get(key, 0) < v:
                b.r[key] = v
        for b in writes:
            b.w = (key, v)
            b.r = {}

    def barrier(self, engs=("pe", "act", "dve")):
        snap = {e: self.cnt[e] for e in engs}
        for e in engs:
            waits = []
            for e2 in engs:
                if e2 == e or snap[e2] == 0:
                    continue
                if self.waited[e].get(e2, 0) < snap[e2]:
                    waits.append((e2, snap[e2]))
                    self.waited[e][e2] = snap[e2]
            if waits:
                self.q[e].append((waits, None, None, 0))

    def wait_all(self, eng, bufs):
        waits = self._waits(eng, bufs, bufs)
        self.q[eng].append((waits, None, None, 0))

    def emit(self):
        nc = self.nc
        P = self

        targets = {e: set() for e in P.ENG}
        for name in P.ENG:
            for waits, fn, dkey, idx in P.q[name]:
                for k, v in waits:
                    if k in targets:
                        targets[k].add(v)
        rank = {}
        for e_, st in targets.items():
            rank[e_] = {v: i + 1 for i, v in enumerate(sorted(st))}

        def replay(name, e):
            tg = targets[name]
            for waits, fn, dkey, idx in P.q[name]:
                for k, v in waits:
                    e.wait_ge(P._semobj(k), rank[k][v] if k in rank else v)
                if fn is None:
                    continue
                ins = fn(e)
                if dkey is not None:
                    ins.then_inc(P.dsem[dkey], 16)
                elif idx in tg:
                    ins.then_inc(P.sem[name], 1)

        with nc.Block() as block:

            @block.tensor
            def _(e):
                replay("pe", e)

            @block.scalar
            def _(e):
                replay("act", e)

            @block.vector
            def _(e):
                replay("dve", e)

            @block.gpsimd
            def _(e):
                replay("pool", e)

            @block.sync
            def _(e):
                replay("sp", e)


def build_program(NT, NL, T_in, stop_after=99):
    nc = bass.Bass("TRN2", target_bir_lowering=False)
    dt = nc.dram_tensor
    xT = dt("xT", [NCH, 128, T_in], F32, kind="ExternalInput").ap()
    pT = dt("pT", [DEPTH, 2, 128, T_in], F32, kind="ExternalInput").ap()
    oT = dt("oT", [NCH, 128, T_in], F32, kind="ExternalOutput").ap()
    vecd = dt("vec", [128, DEPTH, NV], F32, kind="ExternalInput").ap()
    cbd = dt("cbf", [128, NCB], F32, kind="ExternalInput").ap()
    cfd = dt("cf32", [128, NCF], F32, kind="ExternalInput").ap()
    wLB = dt("wLB", [NL, 128, 16, 320], F32, kind="ExternalInput").ap()
    wPB = dt("wPB", [NL, 2, 128, 16, 512], F32, kind="ExternalInput").ap()
    wRKV = dt("wRKV", [NL, 8, 128, 16, 384], F32, kind="ExternalInput").ap()
    wSM = dt("wSM", [NL, 128, 6144], F32, kind="ExternalInput").ap()
    wWO = dt("wWO", [NL, 8, 128, 24, 256], F32, kind="ExternalInput").ap()
    wUP = dt("wUP", [NL, 16, 128, 16, 512], F32, kind="ExternalInput").ap()
    wDN = dt("wDN", [NL, 4, 4, 128, 16, 512], F32, kind="ExternalInput").ap()
    wGT = dt("wGT", [NL, 4, 128, 16, 512], F32, kind="ExternalInput").ap()
    wPJ = dt("wPJ", [NL, 128, 2, 2048], F32, kind="ExternalInput").ap()

    cLB = dt("cLB", [NL, 128, 16, 320], BF16, kind="Internal").ap()
    cPB = dt("cPB", [NL, 2, 128, 16, 512], BF16, kind="Internal").ap()
    cRKV = dt("cRKV", [NL, 8, 128, 16, 384], BF16, kind="Internal").ap()
    cWO = dt("cWO", [NL, 8, 128, 24, 256], BF16, kind="Internal").ap()
    cUP = dt("cUP", [NL, 16, 128, 16, 512], BF16, kind="Internal").ap()
    cDN = dt("cDN", [NL, 4, 4, 128, 16, 512], BF16, kind="Internal").ap()
    cGT = dt("cGT", [NL, 4, 128, 16, 512], BF16, kind="Internal").ap()

    P = Prog(nc)
    from contextlib import ExitStack

    with ExitStack() as es:
        def sb(name, shape, dtype):
            return es.enter_context(nc.sbuf_tensor(name, shape, dtype))

        X = sb("X", [128, NCH, TT], F32)
        H = sb("H", [128, NCH, TT], BF16)
        WS = [sb(f"WS{i}", [128, WSLOT], BF16) for i in range(NWS)]
        SMW = sb("SMW", [128, 6144], BF16)
        MIX = sb("MIX", [128, 24, TT], BF16)
        VF = sb("VF", [128, 8, TT], F32)
        SCR = sb("SCR", [128, 64 * TT], BF16)
        VEC = sb("VEC", [128, DEPTH, NV], F32)
        CB = sb("CB", [128, NCB], BF16)
        CF = sb("CF", [128, NCF], F32)
        SQ = sb("SQ", [128, 2, TT], BF16)
        RSTD = sb("RSTD", [128, TT], F32)
        CARRY = sb("CARRY", [128, DEPTH, 28], F32)
        CARRYP = sb("CARRYP", [128, DEPTH, 8, 16], BF16)
        ST = sb("ST", [64, DEPTH, 16, 64], BF16)
        LWA = sb("LWA", [128, TT], BF16)
        LG0 = sb("LG0", [128, TT], BF16)
        LG1 = sb("LG1", [32, TT], BF16)
        LVR = sb("LVR", [32, TT], BF16)
        ZP = sb("ZP", [128, 8, 16 + TT], BF16)
        QC = sb("QC", [128, 1, 512], BF16)
        QH = sb("QH", [16, 1, 512], BF16)
        TMALL = sb("TMALL", [64, 4, NSUB, 4, 128], BF16)
        DGE = sb("DGE", [128, 4, NSUB, 64], BF16)
        RKW = sb("RKW", [128, 8, 64], BF16)
        CH = sb("CH", [64, 14, 8, 64], BF16)
        Y32 = sb("Y32", [64, 8, TT], F32)
        OT = sb("OT", [64, 6, TT], F32)
        PTB = sb("PTB", [128, 2, TT], BF16)
        PJW = sb("PJW", [128, 2, 2048], BF16)
        TBUF = sb("TBUF", [128, 2, TT + 1], F32)
        PS = [es.enter_context(nc.psum_tensor(f"PS{i}", [128, 512], F32)) for i in range(8)]

        bPS = [Buf(f"ps{i}") for i in range(8)]
        bX = [Buf(f"X{c}") for c in range(NCH)]
        bH = [Buf(f"H{c}") for c in range(NCH)]
        bWS = [Buf(f"ws{i}") for i in range(NWS)]
        bSMW = Buf("smw")
        bMIX = [Buf(f"mix{c}") for c in range(24)]
        bVF = [Buf(f"vf{c}") for c in range(8)]
        bVEC, bCB, bCF = Buf("vec"), Buf("cb"), Buf("cf")
        bSQ = [Buf("sq0"), Buf("sq1")]
        bRSTD = Buf("rstd")
        bCARRY = [[Buf(f"cy{l}_{i}") for i in range(28)] for l in range(DEPTH)]
        bCARRYP = [Buf(f"cyp{l}") for l in range(DEPTH)]
        bST = [[Buf(f"st{l}_{h}") for h in range(16)] for l in range(DEPTH)]
        bLWA, bLG0, bLG1, bLVR = Buf("lwa"), Buf("lg0"), Buf("lg1"), Buf("lvr")
        bZP = [Buf(f"zp{c}") for c in range(8)]
        bQC = [Buf("qc0"), Buf("qc1")]
        bQH = [Buf("qh0"), Buf("qh1")]
        bTM = [[Buf(f"tm{m}_{s}") for s in range(NSUB)] for m in range(4)]
        bDGE = [Buf(f"dge{m}") for m in range(4)]
        bRKW = Buf("rkw")
        bCH = [Buf(f"ch{i}") for i in range(14)]
        bY32 = [Buf(f"y{h}") for h in range(8)]
        bOT = [Buf(f"ot{i}") for i in range(8)]
        bPTB = Buf("ptb")
        bPJW = Buf("pjw")
        bOST = [Buf("ost0"), Buf("ost1")]
        bTB = [Buf("tb0"), Buf("tb1")]
        bCV = [Buf(f"cv{l}") for l in range(DEPTH)]
        bPHASE = Buf("phase")

        HID = SCR[:, :].rearrange("p (c t) -> p c t", t=TT)
        bHID = [Buf(f"hid{c}") for c in range(64)]
        _scr_off = [0]

        def scr_bf(n):
            o = _scr_off[0]
            _scr_off[0] += n
            return SCR[:, o:o + n]

        def scr_f32(n):
            return scr_bf(2 * n).bitcast(F32)

        NTMP = 18
        TMP = [scr_f32(TT) for _ in range(NTMP)]
        bTMP = [Buf(f"tmp{i}") for i in range(NTMP)]
        TMPB = [scr_bf(TT) for _ in range(4)]
        bTMPB = [Buf(f"tmpb{i}") for i in range(4)]
        GOP = {}
        bGOP = {}
        for nm in ("AT", "BT", "KT", "RT", "RK", "VB"):
            GOP[nm] = scr_bf(4 * TT).rearrange("p (m t) -> p m t", t=TT)
            bGOP[nm] = [Buf(f"{nm}{m}") for m in range(4)]
        assert _scr_off[0] <= 64 * TT, _scr_off[0]

        def ACT(out, in_, func, reads, writes, bias=None, scale=None):
            kw = {}
            if bias is not None:
                kw["bias"] = bias
            if scale is not None:
                kw["scale"] = scale
            P.op("act", lambda e: e.activation(out=out, in_=in_, func=func, **kw), reads, writes)

        def TTT(out, in0, in1, op, reads, writes, eng="dve"):
            P.op(eng, lambda e: e.tensor_tensor(out=out, in0=in0, in1=in1, op=op), reads, writes)

        def STT(out, in0, scalar, in1, op0, op1, reads, writes):
            P.op("dve", lambda e: e.scalar_tensor_tensor(out=out, in0=in0, scalar=scalar, in1=in1, op0=op0, op1=op1), reads, writes)

        def TS(out, in0, s1, s2, op0, op1, reads, writes):
            if op1 is None:
                P.op("dve", lambda e: e.tensor_scalar(out=out, in0=in0, scalar1=s1, scalar2=None, op0=op0), reads, writes)
            else:
                P.op("dve", lambda e: e.tensor_scalar(out=out, in0=in0, scalar1=s1, scalar2=s2, op0=op0, op1=op1), reads, writes)

        def CP(out, in_, reads, writes, eng="dve"):
            P.op(eng, lambda e: e.tensor_copy(out=out, in_=in_), reads, writes)

        def MM(out, lhsT, rhs, start, stop, reads, writes):
            P.op("pe", lambda e: e.matmul(out, lhsT=lhsT, rhs=rhs, start=start, stop=stop), reads, writes)

        def TR(out, in_, ident, reads, writes):
            P.op("pe", lambda e: e.transpose(out, in_, ident), reads, writes)

        def RECIP(out, in_, reads, writes):
            P.op("dve", lambda e: e.reciprocal(out=out, in_=in_), reads, writes)

        def MEMSET(ap, val, writes, eng="dve"):
            P.op(eng, lambda e: e.memset(ap, val), (), writes)

        def DMA(q, key, pairs, reads, writes):
            fns = [(lambda e, o=o, i=i: e.dma_start(out=o, in_=i)) for (o, i) in pairs]
            P.dma(q, key, fns, reads, writes)

        def vcol(l, c, rows=128):
            return VEC[0:rows, l, c:c + 1]

        ws_ptr = [0]

        def wload(src, n_free, view_fn, l):
            i = ws_ptr[0] % NWS
            ws_ptr[0] += 1
            dst = view_fn(WS[i][:, 0:n_free])
            DMA("sp", f"w{i}", [(dst, src)], [bCV[l]], [bWS[i]])
            return i, dst

        def convert_layer(l):
            pairs = [(cLB[l], wLB[l])]
            pairs += [(cPB[l, i], wPB[l, i]) for i in range(2)]
            pairs += [(cRKV[l, i], wRKV[l, i]) for i in range(8)]
            pairs += [(cWO[l, i], wWO[l, i]) for i in range(8)]
            pairs += [(cUP[l, i], wUP[l, i]) for i in range(16)]
            pairs += [(cDN[l, i, j], wDN[l, i, j]) for i in range(4) for j in range(4)]
            pairs += [(cGT[l, i], wGT[l, i]) for i in range(4)]
            DMA("pool", f"cv{l}", pairs, [], [bCV[l]])

        DMA("sp", "vec", [(VEC[:, :, :], vecd[:, :, :])], [], [bVEC])
        DMA("sp", "cf", [(CF[:, :], cfd[:, :])], [], [bCF])
        DMA("pool", "cb", [(CB[:, :], cbd[:, :])], [], [bCB])
        for l in range(NL):
            TS(VEC[:, l, V_OMM:V_OMM + 28], VEC[:, l, V_MU:V_MU + 28], -1.0, 1.0, ALU.mult, ALU.add, [bVEC], [bVEC])
            TS(VEC[:, l, V_OKA:V_OKA + 8], VEC[:, l, V_KA:V_KA + 8], -1.0, 1.0, ALU.mult, ALU.add, [bVEC], [bVEC])
        MEMSET(CARRY[:, :, :], 0.0, [b for row in bCARRY for b in row])
        MEMSET(CARRYP[:, :, :, :], 0.0, bCARRYP)
        MEMSET(ST[:, :, :, :], 0.0, [b for row in bST for b in row])
        MEMSET(MIX[:, :, :], 0.0, bMIX)

        convert_layer(0)
        ONESM = CB[:, CB_ONESM:CB_ONESM + 128]
        IDENTB = CB[:, CB_ID:CB_ID + 128]
        BONES = CB[:, CB_BONES:CB_BONES + 128]
        IDB = CB[:, CB_IDB:CB_IDB + 64]

        def rmsnorm(l, gcol0):
            for c in range(NCH):
                ACT(SQ[:, c % 2, :], X[:, c, :], AF.Square, [bX[c]], [bSQ[c % 2]])
                MM(PS[7][:, 0:TT], ONESM, SQ[:, c % 2, :], c == 0, c == NCH - 1, [bSQ[c % 2], bCB], [bPS[7]])
            ACT(RSTD[:, :], PS[7][:, 0:TT], AF.Sqrt, [bPS[7], bCF], [bRSTD], bias=CF[:, CF_EPSN:CF_EPSN + 1])
            RECIP(RSTD[:, :], RSTD[:, :], [bRSTD], [bRSTD])

        def norm_to_H(l, gcol0):
            rmsnorm(l, gcol0)
            for c in range(NCH):
                STT(H[:, c, :], X[:, c, :], vcol(l, gcol0 + c), RSTD[:, :], ALU.mult, ALU.mult,
                    [bX[c], bRSTD, bVEC], [bH[c]])

        tb_ptr = [0]

        def shift_evac(l, ps_ap, pbuf, rows, cidx, out_ap, out_bufs, extra_reads=()):
            k = tb_ptr[0] % 2
            tb_ptr[0] += 1
            TA = TMP[16 + k]
            bTA = bTMP[16 + k]
            TB = TBUF[:, k, :]
            CP(TB[0:rows, 0:1], CARRY[0:rows, l, cidx:cidx + 1], [bCARRY[l][cidx], bPHASE], [bTB[k]])
            ACT(TA[0:rows, :], ps_ap, AF.Identity, [pbuf, bVEC, bPHASE], [bTA], scale=vcol(l, V_OMM + cidx, rows))
            ACT(TB[0:rows, 1:TT + 1], ps_ap, AF.Identity, [pbuf, bVEC], [bTB[k]], scale=vcol(l, V_MU + cidx, rows))
            TTT(out_ap, TA[0:rows, :], TB[0:rows, 0:TT], ALU.add, [bTA, bTB[k], bPHASE] + list(extra_reads), out_bufs)
            CP(CARRY[0:rows, l, cidx:cidx + 1], TB[0:rows, TT:TT + 1], [bTB[k]], [bCARRY[l][cidx]])

        def big_mm(ps_ap, pbuf, wview_fn, wbuf, rhs_fn, rhs_bufs, nk):
            for k in range(nk):
                MM(ps_ap, wview_fn(k), rhs_fn(k), k == 0, k == nk - 1, [wbuf, rhs_bufs[k]], [pbuf])

        for tile in range(NT):
            t0 = tile * TT
            DMA("sp", "x", [(X[:, :, :], xT.rearrange("c p t -> p c t")[:, :, t0:t0 + TT])], [], bX)
            for l in range(NL):
                first_tile = tile == 0
                DMA("pool", "ptb", [(PTB[:, :, :], pT[l].rearrange("k p t -> p k t")[:, :, t0:t0 + TT])], [], [bPTB])
                norm_to_H(l, V_ATTN)
                if stop_after < 1:
                    continue
                si, wv = wload(cLB[l], 16 * 320, lambda a: a.rearrange("p (k c) -> p k c", c=320), l)
                P.barrier()
                specs = [(0, 128, 24, 0), (128, 128, 25, 1), (256, 32, 26, 2)]
                if l > 0:
                    specs.append((288, 32, 27, 3))
                for (c0, rows, cidx, bank) in specs:
                    big_mm(PS[bank][0:rows, 0:TT], bPS[bank], lambda k, c0=c0, rows=rows: wv[:, k, c0:c0 + rows], bWS[si],
                           lambda k: H[:, k, :], bH, NCH)
                for (c0, rows, cidx, bank) in specs:
                    zs = TMP[15]
                    shift_evac(l, PS[bank][0:rows, 0:TT], bPS[bank], rows, cidx, zs[0:rows, :], [bTMP[15]])
                    if cidx == 24:
                        ACT(LWA[0:64, :], zs[0:64, :], AF.Tanh, [bTMP[15]], [bLWA])
                        ACT(LWA[64:128, :], zs[64:128, :], AF.Identity, [bTMP[15]], [bLWA])
                    elif cidx == 25:
                        ACT(LG0[:, :], zs[:, :], AF.Sigmoid, [bTMP[15]], [bLG0])
                    elif cidx == 26:
                        ACT(LG1[:, :], zs[0:32, :], AF.Sigmoid, [bTMP[15]], [bLG1])
                    else:
                        ACT(LVR[:, :], zs[0:32, :], AF.Identity, [bTMP[15]], [bLVR])
                if stop_after < 2:
                    continue
                DMA("pool", "smw", [(SMW[:, :], wSM[l])], [], [bSMW])
                DMA("pool", "pjw", [(PJW[:, :, :], wPJ[l])], [], [bPJW])
                if first_tile and l + 1 < NL:
                    convert_layer(l + 1)
                WUP = SMW[0:64, 0:1024]
                AUP = SMW[64:128, 0:1024]
                GUP0 = SMW[:, 1024:2048]
                GUP1 = SMW[0:32, 2048:3072]
                VUP = SMW[0:32, 3072:4096]
                POOLW = SMW[:, 4096:6144].rearrange("p (g k d) -> p g k d", g=4, k=2)
                CP(ZP[:, :, 0:16], CARRYP[:, l, :, :], [bCARRYP[l]], bZP)
                for pb in range(2):
                    si, wv = wload(cPB[l, pb], 16 * 512, lambda a: a.rearrange("p (k c) -> p k c", c=512), l)
                    for j in range(4):
                        c = pb * 4 + j
                        bank = 4 + j
                        big_mm(PS[bank][:, 0:TT], bPS[bank], lambda k, j=j: wv[:, k, j * 128:(j + 1) * 128], bWS[si],
                               lambda k: H[:, k, :], bH, NCH)
                        ACT(ZP[:, c, 16:16 + TT], PS[bank][:, 0:TT], AF.Copy, [bPS[bank]], [bZP[c]])
                CP(CARRYP[:, l, :, :], ZP[:, :, TT:TT + 16], bZP, [bCARRYP[l]])
                if stop_after < 3:
                    continue
                for gp in range(2):
                    for tb in range(NTB):
                        for gg in range(2):
                            g = 2 * gp + gg
                            for kk in range(2):
                                MM(PS[4][:, gg * 256:(gg + 1) * 256], ZP[:, 2 * g + kk, 16 + tb * 128:16 + (tb + 1) * 128],
                                   POOLW[:, g, kk, :], kk == 0, kk == 1, [bZP[2 * g + kk], bSMW], [bPS[4]])
                        for gg in range(2):
                            g = 2 * gp + gg
                            for kk in range(2):
                                MM(PS[5][0:16, gg * 256:(gg + 1) * 256], ZP[:, 2 * g + kk, tb * 128:tb * 128 + 16],
                                   POOLW[:, g, kk, :], kk == 0, kk == 1, [bZP[2 * g + kk], bSMW], [bPS[5]])
                        qi = 0
                        ACT(QC[:, qi, :], PS[4][:, :], AF.Copy, [bPS[4]], [bQC[qi]])
                        CP(QH[:, qi, :], PS[5][0:16, :], [bPS[5]], [bQH[qi]])
                        for dmi in range(4):
                            g = 2 * gp + dmi // 2
                            pmc = CB_PMF if (first_tile and tb == 0) else CB_PMC
                            MM(PS[dmi][:, tb * 128:(tb + 1) * 128], QH[0:16, qi, dmi * 128:(dmi + 1) * 128],
                               CB[0:16, CB_PMH + g * 128:CB_PMH + (g + 1) * 128], True, False, [bQH[qi], bCB], [bPS[dmi]])
                            MM(PS[dmi][:, tb * 128:(tb + 1) * 128], QC[:, qi, dmi * 128:(dmi + 1) * 128],
                               CB[:, pmc + g * 128:pmc + (g + 1) * 128], False, True, [bQC[qi], bCB], [bPS[dmi]])
                    for dmi in range(4):
                        dm = gp * 4 + dmi
                        ACT(MIX[:, dm, :], PS[dmi][:, 0:TT], AF.Identity, [bPS[dmi], bVEC], [bMIX[dm]], scale=vcol(l, V_PSC + dm))
                if stop_after < 4:
                    continue
                CP(RKW[:, :, :], VEC[:, l, V_RK:V_RK + 8].unsqueeze(2).to_broadcast([128, 8, 64]), [bVEC], [bRKW])
                for grp in range(2):
                    for mloc in range(4):
                        m = grp * 4 + mloc
                        rwkv_prep = None
                        MM(PS[4][:, 0:TT], WUP[:, m * 128:(m + 1) * 128], LWA[0:64, :], True, True, [bSMW, bLWA], [bPS[4]])
                        MM(PS[5][:, 0:TT], AUP[:, m * 128:(m + 1) * 128], LWA[64:128, :], True, True, [bSMW, bLWA], [bPS[5]])
                        LD, A32, SG, CS, CSM, IGAM, GAM1, GAM, DREV, GREV = TMP[0:10]
                        bLD, bA32, bSG, bCS, bCSM, bIGAM, bGAM1, bGAM, bDREV, bGREV = bTMP[0:10]
                        ACT(LD, PS[4][:, 0:TT], AF.Sigmoid, [bPS[4], bVEC, bPHASE], [bLD], bias=vcol(l, V_W0 + m))
                        ACT(A32, PS[5][:, 0:TT], AF.Sigmoid, [bPS[5], bVEC, bPHASE], [bA32], bias=vcol(l, V_A0 + m))
                        if l > 0:
                            MM(PS[6][:, 0:TT], VUP[:, m * 128:(m + 1) * 128], LVR[:, :], True, True, [bSMW, bLVR], [bPS[6]])
                            ACT(SG, PS[6][:, 0:TT], AF.Sigmoid, [bPS[6], bVEC, bPHASE], [bSG], bias=vcol(l, V_V0 + m))
                        P.op("dve", lambda e, CS=CS, LD=LD: e.tensor_tensor_scan(out=CS, data0=CF[:, CF_RESET:CF_RESET + TT], data1=LD,
                                                                                 initial=0.0, op0=ALU.mult, op1=ALU.add),
                             [bCF, bLD, bPHASE], [bCS])
                        ACT(IGAM, CS, AF.Exp, [bCS, bPHASE], [bIGAM], scale=C0)
                        ACT(GAM, CS, AF.Exp, [bCS, bPHASE], [bGAM], scale=-C0)
                        TTT(CSM, CS, LD, ALU.subtract, [bCS, bLD, bPHASE], [bCSM])
                        ACT(GAM1, CSM, AF.Exp, [bCSM, bPHASE], [bGAM1], scale=-C0)
                        CS3 = CS.rearrange("p (s t) -> p s t", t=64)
                        TTT(DREV.rearrange("p (s t) -> p s t", t=64), CS3[:, :, 63:64].to_broadcast([128, NSUB, 64]), CS3,
                            ALU.subtract, [bCS, bPHASE], [bDREV])
                        ACT(GREV, DREV, AF.Exp, [bDREV, bPHASE], [bGREV], scale=-C0)
                        GAM3 = GAM.rearrange("p (s t) -> p s t", t=64)
                        TTT(DGE[:, mloc, :, :], CF[:, CF_ID2:CF_ID2 + 64].unsqueeze(1).to_broadcast([128, NSUB, 64]),
                            GAM3[:, :, 63:64].to_broadcast([128, NSUB, 64]), ALU.mult, [bCF, bGAM], [bDGE[mloc]])
                        si, wv = wload(cRKV[l, m], 16 * 384, lambda a: a.rearrange("p (k c) -> p k c", c=384), l)
                        for j in range(3):
                            big_mm(PS[4 + j][:, 0:TT], bPS[4 + j], lambda k, j=j: wv[:, k, j * 128:(j + 1) * 128], bWS[si],
                                   lambda k: H[:, k, :], bH, NCH)
                        R32, K32, V32, RN, KKN, B32 = TMP[10:16][0], TMP[11], TMP[12], TMP[13], TMP[14], TMP[15]
                        bR32, bK32, bV32, bRN, bKKN, bB32 = bTMP[10], bTMP[11], bTMP[12], bTMP[13], bTMP[14], bTMP[15]
                        shift_evac(l, PS[4][:, 0:TT], bPS[4], 128, 0 + m, R32, [bR32])
                        shift_evac(l, PS[5][:, 0:TT], bPS[5], 128, 8 + m, K32, [bK32])
                        shift_evac(l, PS[6][:, 0:TT], bPS[6], 128, 16 + m, V32, [bV32])
                        TTT(GOP["RT"][:, mloc, :], R32, GAM, ALU.mult, [bR32, bGAM, bPHASE], [bGOP["RT"][mloc]])
                        KSQ = TMPB[0]
                        ACT(KSQ, K32, AF.Square, [bK32, bVEC, bPHASE], [bTMPB[0]], scale=vcol(l, V_KK + m))
                        MM(PS[7][:, 0:TT], BONES, KSQ, True, True, [bCB, bTMPB[0]], [bPS[7]])
                        ACT(RN, PS[7][:, 0:TT], AF.Sqrt, [bPS[7], bPHASE], [bRN])
                        TS(RN, RN, 1e-12, None, ALU.max, None, [bRN], [bRN])
                        RECIP(RN, RN, [bRN], [bRN])
                        STT(KKN, K32, vcol(l, V_KK + m), RN, ALU.mult, ALU.mult, [bK32, bRN, bVEC, bPHASE], [bKKN])
                        STT(GOP["AT"][:, mloc, :], KKN, -1.0, GAM1, ALU.mult, ALU.mult, [bKKN, bGAM1, bPHASE], [bGOP["AT"][mloc]])
                        TTT(B32, KKN, A32, ALU.mult, [bKKN, bA32, bPHASE], [bB32])
                        TTT(GOP["BT"][:, mloc, :], B32, IGAM, ALU.mult, [bB32, bIGAM, bPHASE], [bGOP["BT"][mloc]])
                        BH = TMPB[1]
                        TTT(BH, B32, GREV, ALU.mult, [bB32, bGREV, bPHASE], [bTMPB[1]])
                        T1 = RN
                        TS(T1, A32, vcol(l, V_KA + m), vcol(l, V_OKA + m), ALU.mult, ALU.add, [bA32, bVEC, bPHASE], [bRN])
                        KP = KKN
                        TTT(KP, K32, T1, ALU.mult, [bK32, bRN, bPHASE], [bKKN])
                        TTT(GOP["KT"][:, mloc, :], KP, IGAM, ALU.mult, [bKKN, bIGAM, bPHASE], [bGOP["KT"][mloc]])
                        KH = TMPB[2]
                        TTT(KH, KP, GREV, ALU.mult, [bKKN, bGREV, bPHASE], [bTMPB[2]])
                        TTT(GOP["RK"][:, mloc, :], R32, KP, ALU.mult, [bR32, bKKN, bPHASE], [bGOP["RK"][mloc]])
                        if l == 0:
                            CP(VF[:, m, :], V32, [bV32, bPHASE], [bVF[m]])
                            CP(GOP["VB"][:, mloc, :], V32, [bV32, bPHASE], [bGOP["VB"][mloc]])
                        else:
                            TTT(B32, VF[:, m, :], V32, ALU.subtract, [bVF[m], bV32, bPHASE], [bB32])
                            TTT(B32, B32, SG, ALU.mult, [bB32, bSG, bPHASE], [bB32])
                            TTT(GOP["VB"][:, mloc, :], V32, B32, ALU.add, [bV32, bB32, bPHASE], [bGOP["VB"][mloc]])
                        srcs = [(GOP["AT"][:, mloc, :], bGOP["AT"][mloc]), (BH, bTMPB[1]), (KH, bTMPB[2]),
                                (GOP["VB"][:, mloc, :], bGOP["VB"][mloc])]
                        for sp in range(NSUB // 2):
                            bank = sp % 2
                            PSB = PS[bank][:, :].bitcast(BF16)
                            for s2 in range(2):
                                s = sp * 2 + s2
                                for qi, (src, sbf) in enumerate(srcs):
                                    TR(PSB[0:64, (s2 * 4 + qi) * 128:(s2 * 4 + qi + 1) * 128], src[:, s * 64:(s + 1) * 64], IDENTB,
                                       [sbf, bCB, bPHASE], [bPS[bank]])
                            CP(TMALL[:, mloc, sp * 2:sp * 2 + 2, :, :].rearrange("p s q c -> p (s q c)"), PSB[0:64, :],
                               [bPS[bank]], [bTM[mloc][sp * 2], bTM[mloc][sp * 2 + 1]])
                    if stop_after < 5:
                        continue
                    MU_S = CF[0:64, CF_MU_S:CF_MU_S + 64].unsqueeze(1).to_broadcast([64, 8, 64])
                    MU_I = CF[0:64, CF_MU_I:CF_MU_I + 64].unsqueeze(1).to_broadcast([64, 8, 64])
                    ML_S = CF[0:64, CF_ML_S:CF_ML_S + 64].unsqueeze(1).to_broadcast([64, 8, 64])
                    ID64 = CF[0:64, CF_ID64:CF_ID64 + 64].unsqueeze(1).to_broadcast([64, 8, 64])
                    bk = [0]
                    bko = [0]

                    def nbank():
                        b = bk[0] % 6
                        bk[0] += 1
                        return b

                    def nbank_o():
                        b = 6 + bko[0] % 2
                        bko[0] += 1
                        return b

                    def pview(b):
                        return PS[b][0:64, :].rearrange("p (h t) -> p h t", t=64)

                    def SELJ(j):
                        return CB[:, CB_SEL + 64 * j:CB_SEL + 64 * (j + 1)]

                    for s in range(NSUB):
                        tc0 = s * 64

                        def fm(nm, hi):
                            j, ml = hi // 4, hi % 4
                            return GOP[nm][64 * j:64 * j + 64, ml, tc0:tc0 + 64], bGOP[nm][ml]

                        def tm(q, hi):
                            j, ml = hi // 4, hi % 4
                            return TMALL[:, ml, s, q, 64 * j:64 * j + 64], bTM[ml][s]

                        (cM, cN, cARB, cAAK, cARK, cP0, cP1, cM2, cN2, cX, cQQ, cPP, cRP, cGG) = range(14)

                        def amat(lname, rname, ci, mask):
                            be = nbank()
                            bo = nbank_o()
                            for hi in range(8):
                                la, lb = fm(lname, hi)
                                ra, rb = fm(rname, hi)
                                b = be if hi < 4 else bo
                                MM(pview(b)[:, hi % 4, :], la, ra, True, True, [lb, rb, bPHASE], [bPS[b]])
                            TTT(CH[:, ci, 0:4, :], pview(be)[:, 0:4, :], mask[:, 0:4, :], ALU.mult, [bPS[be], bCF], [bCH[ci]])
                            TTT(CH[:, ci, 4:8, :], pview(bo)[:, 0:4, :], mask[:, 0:4, :], ALU.mult, [bPS[bo], bCF], [bCH[ci]])

                        amat("BT", "AT", cM, MU_S)
                        amat("AT", "BT", cN, ML_S)
                        amat("BT", "RT", cARB, MU_I)
                        amat("KT", "AT", cAAK, MU_S)
                        amat("KT", "RT", cARK, MU_I)
                        TTT(CH[:, cP0, :, :], CH[:, cM, :, :], ID64, ALU.add, [bCH[cM], bCF], [bCH[cP0]])
                        curM, curN, curP = cM, cN, cP0
                        altM, altN, altP = cM2, cN2, cP1
                        for lev in range(5):
                            bN = nbank()
                            for hi in range(8):
                                MM(pview(bN)[:, hi, :], CH[:, curM, hi, :], CH[:, curN, hi, :], True, True,
                                   [bCH[curM], bCH[curN]], [bPS[bN]])
                            ACT(CH[:, altN, :, :], pview(bN), AF.Copy, [bPS[bN]], [bCH[altN]])
                            if lev < 4:
                                bM = nbank()
                                for hi in range(8):
                                    MM(pview(bM)[:, hi, :], CH[:, curN, hi, :], CH[:, curM, hi, :], True, True,
                                       [bCH[curM], bCH[curN]], [bPS[bM]])
                                ACT(CH[:, altM, :, :], pview(bM), AF.Copy, [bPS[bM]], [bCH[altM]])
                            bP = nbank()
                            for hi in range(8):
                                MM(pview(bP)[:, hi, :], CH[:, altN, hi, :], CH[:, curP, hi, :], True, True,
                                   [bCH[altN], bCH[curP]], [bPS[bP]])
                            TTT(CH[:, altP, :, :], CH[:, curP, :, :], pview(bP), ALU.add, [bCH[curP], bPS[bP]], [bCH[altP]])
                            curM, altM = altM, curM
                            curN, altN = altN, curN
                            curP, altP = altP, curP
                        cTt = curP
                        b = nbank()
                        for hi in range(8):
                            va, vb = tm(3, hi)
                            MM(pview(b)[:, hi, :], CH[:, cAAK, hi, :], va, True, True, [bCH[cAAK], vb], [bPS[b]])
                        ACT(CH[:, cX, :, :], pview(b), AF.Copy, [bPS[b]], [bCH[cX]])
                        b = nbank()
                        for hi in range(8):
                            MM(pview(b)[:, hi, :], CH[:, cTt, hi, :], CH[:, cX, hi, :], True, True, [bCH[cTt], bCH[cX]], [bPS[b]])
                        ACT(CH[:, cQQ, :, :], pview(b), AF.Copy, [bPS[b]], [bCH[cQQ]])
                        b = nbank()
                        for hi in range(8):
                            aa, ab = tm(0, hi)
                            MM(pview(b)[:, hi, :], CH[:, cTt, hi, :], aa, True, True, [bCH[cTt], ab], [bPS[b]])
                        CP(CH[:, cPP, :, :], pview(b), [bPS[b]], [bCH[cPP]])
                        b = nbank()
                        for hi in range(8):
                            j, ml = hi // 4, hi % 4
                            MM(pview(b)[:, hi, :], SELJ(j), GOP["RT"][:, ml, tc0:tc0 + 64], True, False, [bCB, bGOP["RT"][ml]], [bPS[b]])
                            MM(pview(b)[:, hi, :], CH[:, cPP, hi, :], CH[:, cARB, hi, :], False, True, [bCH[cPP], bCH[cARB]], [bPS[b]])
                        ACT(CH[:, cRP, :, :], pview(b), AF.Copy, [bPS[b]], [bCH[cRP]])
                        b = nbank()
                        for hi in range(8):
                            j, ml = hi // 4, hi % 4
                            ba, bb = tm(1, hi)
                            MM(pview(b)[:, hi, :], SELJ(j), DGE[:, ml, s, :], True, False, [bCB, bDGE[ml]], [bPS[b]])
                            MM(pview(b)[:, hi, :], CH[:, cPP, hi, :], ba, False, True, [bCH[cPP], bb], [bPS[b]])
                        CP(CH[:, cGG, :, :], pview(b), [bPS[b]], [bCH[cGG]])
                        b = nbank()
                        stb = [bST[l][grp * 8 + hi] for hi in range(8)]
                        for hi in range(8):
                            h = grp * 8 + hi
                            va, vb = tm(3, hi)
                            MM(pview(b)[:, hi, :], CH[:, cQQ, hi, :], CH[:, cARB, hi, :], True, False, [bCH[cQQ], bCH[cARB]], [bPS[b]])
                            MM(pview(b)[:, hi, :], va, CH[:, cARK, hi, :], False, False, [vb, bCH[cARK]], [bPS[b]])
                            MM(pview(b)[:, hi, :], ST[:, l, h, :], CH[:, cRP, hi, :], False, True, [stb[hi], bCH[cRP]], [bPS[b]])
                        ACT(Y32[:, :, tc0:tc0 + 64], pview(b), AF.Copy, [bPS[b]], bY32)
                        b = nbank()
                        for hi in range(8):
                            h = grp * 8 + hi
                            ba, bb = tm(1, hi)
                            ka, kb = tm(2, hi)
                            va, vb = tm(3, hi)
                            MM(pview(b)[:, hi, :], ba, CH[:, cQQ, hi, :], True, False, [bb, bCH[cQQ]], [bPS[b]])
                            MM(pview(b)[:, hi, :], ka, va, False, False, [kb, vb], [bPS[b]])
                            MM(pview(b)[:, hi, :], CH[:, cGG, hi, :], ST[:, l, h, :], False, True, [bCH[cGG], stb[hi]], [bPS[b]])
                        CP(ST[:, l, grp * 8:grp * 8 + 8, :], pview(b), [bPS[b]], stb)
                    if stop_after < 6:
                        continue
                    for hh in range(8):
                        j, ml = hh // 4, hh % 4
                        h = grp * 8 + 2 * ml + j
                        bc_, bv_ = (2, 3) if j == 0 else (6, 7)
                        yh = Y32[:, hh, :]
                        YSQ, MUt, T1o, RS, Dd, CBt = [OT[:, i, :] for i in range(6)]
                        ACT(YSQ, yh, AF.Square, [bY32[hh]], [bOT[0]])
                        O64 = CF[0:64, CF_ONES64:CF_ONES64 + 64]
                        MM(PS[0][0:64, 0:TT], O64, yh, True, True, [bCF, bY32[hh]], [bPS[0]])
                        MM(PS[1][0:64, 0:TT], O64, YSQ, True, True, [bCF, bOT[0]], [bPS[1]])
                        ACT(MUt, PS[0][0:64, 0:TT], AF.Copy, [bPS[0]], [bOT[1]])
                        TTT(T1o, MUt, MUt, ALU.mult, [bOT[1]], [bOT[2]])
                        TTT(T1o, PS[1][0:64, 0:TT], T1o, ALU.subtract, [bPS[1], bOT[2]], [bOT[2]])
                        ACT(RS, T1o, AF.Sqrt, [bOT[2], bCF], [bOT[3]], bias=CF[0:64, CF_EPSG:CF_EPSG + 1])
                        RECIP(RS, RS, [bOT[3]], [bOT[3]])
                        TTT(Dd, yh, MUt, ALU.subtract, [bY32[hh], bOT[1]], [bOT[4]])
                        TTT(Dd, Dd, RS, ALU.mult, [bOT[4], bOT[3]], [bOT[4]])
                        TS(Dd, Dd, VEC[0:64, l, V_GNG + h:V_GNG + h + 1], VEC[0:64, l, V_GNB + h:V_GNB + h + 1], ALU.mult, ALU.add,
                           [bOT[4], bVEC], [bOT[4]])
                        MM(PS[bc_][0:64, 0:TT], RKW[64 * j:64 * j + 64, grp * 4 + ml, :], GOP["RK"][64 * j:64 * j + 64, ml, :], True, True,
                           [bRKW, bGOP["RK"][ml], bPHASE], [bPS[bc_]])
                        MM(PS[bv_][0:64, 0:TT], SELJ(j), GOP["VB"][:, ml, :], True, True,
                           [bCB, bGOP["VB"][ml], bPHASE], [bPS[bv_]])
                        ACT(CBt, PS[bc_][0:64, 0:TT], AF.Copy, [bPS[bc_]], [bOT[5]])
                        TTT(CBt, CBt, PS[bv_][0:64, 0:TT], ALU.mult, [bOT[5], bPS[bv_]], [bOT[5]])
                        TTT(Dd, Dd, CBt, ALU.add, [bOT[4], bOT[5]], [bOT[4]])
                        gb = 4 + hh % 2
                        MM(PS[gb][0:64, 0:TT], GUP0[:, h * 64:(h + 1) * 64], LG0[:, :], True, False, [bSMW, bLG0], [bPS[gb]])
                        MM(PS[gb][0:64, 0:TT], GUP1[:, h * 64:(h + 1) * 64], LG1[:, :], False, True, [bSMW, bLG1], [bPS[gb]])
                        TTT(MIX[0:64, 8 + h, :], Dd, PS[gb][0:64, 0:TT], ALU.mult, [bOT[4], bPS[gb]], [bMIX[8 + h]])
                if stop_after < 7:
                    continue
                for cb in range(8):
                    si, wv = wload(cWO[l, cb], 24 * 256, lambda a: a.rearrange("p (k c) -> p k c", c=256), l)
                    for oc2 in range(2):
                        oc = cb * 2 + oc2
                        bank = oc % 4
                        for k in range(8):
                            MM(PS[bank][:, 0:TT], wv[:, k, oc2 * 128:(oc2 + 1) * 128], MIX[:, k, :], k == 0, False,
                               [bWS[si], bMIX[k]], [bPS[bank]])
                        for h in range(16):
                            MM(PS[bank][:, 0:TT], wv[0:64, 8 + h, oc2 * 128:(oc2 + 1) * 128], MIX[0:64, 8 + h, :], False, h == 15,
                               [bWS[si], bMIX[8 + h]], [bPS[bank]])
                        TTT(X[:, oc, :], X[:, oc, :], PS[bank][:, 0:TT], ALU.add, [bX[oc], bPS[bank]], [bX[oc]])
                if stop_after < 8:
                    continue
                norm_to_H(l, V_MLP)
                P.barrier()
                first_hid = False
                for ub in range(16):
                    si, wv = wload(cUP[l, ub], 16 * 512, lambda a: a.rearrange("p (k c) -> p k c", c=512), l)
                    for hc4 in range(4):
                        hc = ub * 4 + hc4
                        bank = 4 + hc % 4
                        big_mm(PS[bank][:, 0:TT], bPS[bank], lambda k, hc4=hc4: wv[:, k, hc4 * 128:(hc4 + 1) * 128], bWS[si],
                               lambda k: H[:, k, :], bH, NCH)
                        k2 = hc % 2
                        ACT(SQ[:, k2, :], PS[bank][:, 0:TT], AF.Relu, [bPS[bank]], [bSQ[k2]])
                        if first_hid:
                            TTT(HID[:, hc, :], SQ[:, k2, :], SQ[:, k2, :], ALU.mult, [bSQ[k2]], [bHID[hc], bPHASE])
                            first_hid = False
                        else:
                            TTT(HID[:, hc, :], SQ[:, k2, :], SQ[:, k2, :], ALU.mult, [bSQ[k2], bPHASE], [bHID[hc]])
                for cb in range(4):
                    for kg in range(4):
                        si, wv = wload(cDN[l, cb, kg], 16 * 512, lambda a: a.rearrange("p (k c) -> p k c", c=512), l)
                        for oc4 in range(4):
                            for k in range(16):
                                MM(PS[oc4][:, 0:TT], wv[:, k, oc4 * 128:(oc4 + 1) * 128], HID[:, kg * 16 + k, :],
                                   kg == 0 and k == 0, kg == 3 and k == 15, [bWS[si], bHID[kg * 16 + k], bPHASE], [bPS[oc4]])
                    for oc4 in range(4):
                        oc = cb * 4 + oc4
                        TTT(X[:, oc, :], X[:, oc, :], PS[oc4][:, 0:TT], ALU.add, [bX[oc], bPS[oc4]], [bX[oc]])
                if stop_after < 9:
                    continue
                norm_to_H(l, V_PLE)
                pj = PJW
                for cb in range(4):
                    si, wv = wload(cGT[l, cb], 16 * 512, lambda a: a.rearrange("p (k c) -> p k c", c=512), l)
                    for oc4 in range(4):
                        oc = cb * 4 + oc4
                        bank = 4 + oc4
                        big_mm(PS[bank][:, 0:TT], bPS[bank], lambda k, oc4=oc4: wv[:, k, oc4 * 128:(oc4 + 1) * 128], bWS[si],
                               lambda k: H[:, k, :], bH, NCH)
                        for k in range(2):
                            MM(PS[oc4][:, 0:TT], pj[:, k, oc * 128:(oc + 1) * 128], PTB[:, k, :], k == 0, k == 1,
                               [bPJW, bPTB], [bPS[oc4]])
                        gt = TBUF[:, oc4 % 2, 0:TT]
                        ACT(gt, PS[bank][:, 0:TT], AF.Sigmoid, [bPS[bank]], [bTB[oc4 % 2]])
                        TTT(gt, gt, PS[oc4][:, 0:TT], ALU.mult, [bTB[oc4 % 2], bPS[oc4]], [bTB[oc4 % 2]])
                        TTT(X[:, oc, :], X[:, oc, :], gt, ALU.add, [bX[oc], bTB[oc4 % 2]], [bX[oc]])
            rmsnorm(0, V_FIN)
            for c in range(NCH):
                k = c % 2
                STT(TBUF[:, k, 0:TT], X[:, c, :], vcol(0, V_FIN + c), RSTD[:, :], ALU.mult, ALU.mult, [bX[c], bRSTD, bVEC], [bTB[k]])
                DMA("sp", f"o{k}", [(oT[c, :, t0:t0 + TT], TBUF[:, k, 0:TT])], [bTB[k]], [])
        P.wait_all("sp", bTB)
        P.emit()
    return nc


def _blk(w, nk):
    return np.ascontiguousarray(w.reshape(nk, 128, w.shape[1]).transpose(1, 0, 2))


def make_consts():
    cb = np.zeros((128, NCB), np.float32)
    cb[:, CB_ID:CB_ID + 128] = np.eye(128)
    cb[:, CB_ONESM:CB_ONESM + 128] = 1.0 / 2048
    blk = np.arange(128) // 64
    cb[:, CB_BONES:CB_BONES + 128] = (blk[:, None] == blk[None, :]).astype(np.float32)
    cb[:, CB_IDB:CB_IDB + 64] = (np.arange(128)[:, None] % 64 == np.arange(64)[None, :]).astype(np.float32)
    for j in range(2):
        cb[:, CB_SEL + j * 64:CB_SEL + (j + 1) * 64] = (np.arange(128)[:, None] == 64 * j + np.arange(64)[None, :]).astype(np.float32)
    t = np.arange(128)
    for g, w in enumerate((2, 4, 8, 16)):
        dlt = t[None, :] - t[:, None]
        band = ((dlt >= 0) & (dlt < w)).astype(np.float32)
        cb[:, CB_PMC + g * 128:CB_PMC + (g + 1) * 128] = band / w - np.eye(128)
        cnt = np.minimum(t + 1, w).astype(np.float32)
        cb[:, CB_PMF + g * 128:CB_PMF + (g + 1) * 128] = band / cnt[None, :] - np.eye(128)
        i = np.arange(16)
        dh = t[None, :] + 16 - i[:, None]
        cb[0:16, CB_PMH + g * 128:CB_PMH + (g + 1) * 128] = ((dh < w)).astype(np.float32) / w
    cf = np.zeros((128, NCF), np.float32)
    r = np.arange(64)
    cf[0:64, CF_MU_S:CF_MU_S + 64] = (r[None, :] > r[:, None])
    cf[0:64, CF_MU_I:CF_MU_I + 64] = (r[None, :] >= r[:, None])
    cf[0:64, CF_ML_S:CF_ML_S + 64] = (r[None, :] < r[:, None])
    cf[0:64, CF_ID64:CF_ID64 + 64] = np.eye(64)
    cf[:, CF_RESET:CF_RESET + TT] = (np.arange(TT) % 64 != 0).astype(np.float32)[None, :]
    cf[:, CF_ID2:CF_ID2 + 64] = (np.arange(128)[:, None] % 64 == np.arange(64)[None, :])
    cf[0:64, CF_ONES64:CF_ONES64 + 64] = 1.0 / 64
    cf[:, CF_EPSN] = NORM_EPS
    cf[:, CF_EPSG] = GN_EPS
    return cb, cf


def prep_weights(inp, NL=DEPTH):
    L = NL
    f = np.float32
    w_in = inp["w_in"]
    out = {}
    wLB = np.zeros((L, 128, 16, 320), f)
    wPB = np.zeros((L, 2, 128, 16, 512), f)
    wRKV = np.zeros((L, 8, 128, 16, 384), f)
    wSM = np.zeros((L, 128, 6144), f)
    wWO = np.zeros((L, 8, 128, 24, 256), f)
    wUP = np.zeros((L, 16, 128, 16, 512), f)
    wDN = np.zeros((L, 4, 4, 128, 16, 512), f)
    wGT = np.zeros((L, 4, 128, 16, 512), f)
    wPJ = np.zeros((L, 128, 2, 2048), f)
    vec = np.zeros((128, DEPTH, NV), f)
    for l in range(L):
        W = w_in[l]
        lb = W[:, 4096:4384]
        if l > 0:
            lb = np.concatenate([lb, inp["w_vres_dn"][l - 1]], axis=1)
        else:
            lb = np.concatenate([lb, np.zeros((D, 32), f)], axis=1)
        wLB[l] = _blk(lb, 16)
        for pb in range(2):
            wPB[l, pb] = _blk(W[:, pb * 512:(pb + 1) * 512], 16)
        for m in range(8):
            cols = np.concatenate([W[:, 1024 + j * 1024 + m * 128:1024 + j * 1024 + (m + 1) * 128] for j in range(3)], axis=1)
            wRKV[l, m] = _blk(cols, 16)
        wSM[l, 0:64, 0:1024] = inp["w_up"][l]
        wSM[l, 64:128, 0:1024] = inp["a_up"][l]
        wSM[l, :, 1024:2048] = inp["g_up"][l][0:128]
        wSM[l, 0:32, 2048:3072] = inp["g_up"][l][128:160]
        if l > 0:
            wSM[l, 0:32, 3072:4096] = inp["v_up"][l - 1]
        pw = inp["pool_w"][l]
        wSM[l, :, 4096:6144] = pw.reshape(4, 2, 128, 256).transpose(2, 0, 1, 3).reshape(128, 2048)
        wo = inp["w_out"][l]
        for cb in range(8):
            sl = wo[:, cb * 256:(cb + 1) * 256]
            wWO[l, cb, :, 0:8, :] = sl[0:1024].reshape(8, 128, 256).transpose(1, 0, 2)
            wWO[l, cb, 0:64, 8:24, :] = sl[1024:2048].reshape(16, 64, 256).transpose(1, 0, 2)
        wu = inp["w_ffn_up"][l]
        for ub in range(16):
            wUP[l, ub] = _blk(wu[:, ub * 512:(ub + 1) * 512], 16)
        wd = inp["w_ffn_down"][l]
        for cb in range(4):
            for kg in range(4):
                wDN[l, cb, kg] = _blk(wd[kg * 2048:(kg + 1) * 2048, cb * 512:(cb + 1) * 512], 16)
        wg = inp["w_ple_gate"][l]
        for cb in range(4):
            wGT[l, cb] = _blk(wg[:, cb * 512:(cb + 1) * 512], 16)
        wPJ[l] = _blk(inp["w_ple_proj"][l], 2)
        vec[:, l, V_ATTN:V_ATTN + 16] = inp["attn_norm"][l].reshape(16, 128).T
        vec[:, l, V_MLP:V_MLP + 16] = inp["mlp_norm"][l].reshape(16, 128).T
        vec[:, l, V_PLE:V_PLE + 16] = inp["ple_norm"][l].reshape(16, 128).T
        vec[:, l, V_FIN:V_FIN + 16] = inp["final_norm"].reshape(16, 128).T
        mu = inp["mu_shift"][l]
        vec[:, l, V_MU:V_MU + 24] = mu[0:3072].reshape(24, 128).T
        vec[:, l, V_MU + 24] = mu[3072:3200]
        vec[:, l, V_MU + 25] = mu[3200:3328]
        vec[0:32, l, V_MU + 26] = mu[3328:3360]
        if l > 0:
            vec[0:32, l, V_MU + 27] = inp["mu_vres"][l - 1]
        vec[:, l, V_PSC:V_PSC + 8] = inp["pool_scale"][l].reshape(8, 128).T
        vec[:, l, V_W0:V_W0 + 8] = inp["w0"][l].reshape(8, 128).T
        vec[:, l, V_A0:V_A0 + 8] = inp["a0"][l].reshape(8, 128).T
        if l > 0:
            vec[:, l, V_V0:V_V0 + 8] = inp["v0"][l - 1].reshape(8, 128).T
        vec[:, l, V_KK:V_KK + 8] = inp["k_k"][l].reshape(8, 128).T
        vec[:, l, V_KA:V_KA + 8] = inp["k_a"][l].reshape(8, 128).T
        vec[:, l, V_RK:V_RK + 8] = inp["r_k"][l].reshape(8, 128).T
        vec[0:64, l, V_GNG:V_GNG + 16] = inp["gn_g"][l].reshape(16, 64).T
        vec[0:64, l, V_GNB:V_GNB + 16] = inp["gn_b"][l].reshape(16, 64).T
    return dict(wLB=wLB, wPB=wPB, wRKV=wRKV, wSM=wSM, wWO=wWO, wUP=wUP, wDN=wDN, wGT=wGT, wPJ=wPJ, vec=vec)


def run_device(inputs, T_run=SEQ, NL=DEPTH, n_cores=8, stop_after=99):
    inp = {k: np.asarray(v, dtype=np.float32) for k, v in inputs.items()}
    wd = prep_weights(inp, NL)
    cb, cf = make_consts()
    NT = T_run // TT
    nc = build_program(NT, NL, T_run, stop_after)
    in_maps = []
    for core in range(n_cores):
        b = (core // 2) % BATCH
        xT = np.ascontiguousarray(inp["x"][b, :T_run].T).reshape(NCH, 128, T_run)
        pT = np.ascontiguousarray(inp["p"][:, b, :T_run].transpose(0, 2, 1)).reshape(DEPTH, 2, 128, T_run)
        m = dict(xT=xT, pT=pT, cbf=cb, cf32=cf)
        m.update(wd)
        in_maps.append(m)
    res = run_bass_kernel_spmd(nc, in_maps, core_ids=list(range(n_cores)))
    outs = []
    for b in range(BATCH):
        o = res.results[2 * b]["oT"] if n_cores == 8 else res.results[min(2 * b, n_cores - 1)]["oT"]
        outs.append(np.asarray(o).reshape(D, T_run).T)
    return np.stack(outs, axis=0).astype(np.float32)


def kernel(**inputs):
    return run_device(inputs)
```
